# Optimizing a Trainium2 kernel written in Bass

```python
import math
import jax, jax.numpy as jnp
from jax import lax
import numpy as np

D_MODEL = 2048
BATCH = 4
SEQ = 4096
DEPTH = 2

HEAD_DIM = 64
DIFF_HEADS = 8
DIFF_QK = 2 * DIFF_HEADS * HEAD_DIM
DIFF_V_DIM = 2 * HEAD_DIM
DIFF_V = DIFF_HEADS * DIFF_V_DIM
Q_BLOCK = 128
SWA_Q_HEADS = 16
SWA_KV_HEADS = 2
SWA_GROUP = SWA_Q_HEADS // SWA_KV_HEADS
SWA_WINDOW = 128
SWA_BLOCK = 128
SWA_Q = SWA_Q_HEADS * HEAD_DIM
SWA_KV = SWA_KV_HEADS * HEAD_DIM
EVEN_SPLITS = [DIFF_QK, DIFF_QK, DIFF_V, SWA_Q, SWA_KV, SWA_KV]
EVEN_IN = sum(EVEN_SPLITS)
MIX_WIDTH = DIFF_V + SWA_Q
LRU_WIDTH = D_MODEL
LRU_BLOCKS = 8
LRU_BLOCK_DIM = LRU_WIDTH // LRU_BLOCKS
CONV_WIDTH = 4
LRU_C = 8.0
D_FF = 4 * D_MODEL
N_EVEN = (DEPTH + 1) // 2
N_ODD = DEPTH // 2
EPS = 1e-6

kernel_name = "hybrid_diffattn_swa_rglru_block"


def rmsnorm(x, g):
    xf = x.astype(jnp.float32)
    y = xf * lax.rsqrt(jnp.mean(xf * xf, axis=-1, keepdims=True) + EPS)
    return (y * g.astype(jnp.float32)).astype(x.dtype)


def diff_attention(q, k, v, lq1, lk1, lq2, lk2, sub_g, lambda_init):
    B, S = q.shape[0], q.shape[1]
    nb = S // Q_BLOCK
    scale = HEAD_DIM ** -0.5
    f32 = jnp.float32
    lam = (jnp.exp(jnp.sum(lq1.astype(f32) * lk1.astype(f32)))
           - jnp.exp(jnp.sum(lq2.astype(f32) * lk2.astype(f32))) + lambda_init)
    qb = q.reshape(B, nb, Q_BLOCK, 2 * DIFF_HEADS, HEAD_DIM).transpose(1, 0, 2, 3, 4)
    kpos = jnp.arange(S)

    def block(args):
        qi, i = args
        s = jnp.einsum('bqhd,bkhd->bhqk', qi, k).astype(f32) * scale
        qpos = i * Q_BLOCK + jnp.arange(Q_BLOCK)
        causal = kpos[None, :] <= qpos[:, None]
        s = jnp.where(causal[None, None], s, -jnp.inf)
        p = jax.nn.softmax(s, axis=-1).reshape(B, DIFF_HEADS, 2, Q_BLOCK, S)
        w = p[:, :, 0] - lam * p[:, :, 1]
        return jnp.einsum('bhqk,bkhe->bqhe', w.astype(v.dtype), v)

    o = lax.map(block, (qb, jnp.arange(nb)))
    o = o.transpose(1, 0, 2, 3, 4).reshape(B, S, DIFF_HEADS, DIFF_V_DIM)
    o = rmsnorm(o, sub_g) * (1.0 - lambda_init)
    return o.reshape(B, S, DIFF_V)


def swa_with_sinks(q, k, v, sinks):
    B, S = q.shape[0], q.shape[1]
    nb = S // SWA_BLOCK
    f32 = jnp.float32
    qb = q.reshape(B, nb, SWA_BLOCK, SWA_KV_HEADS, SWA_GROUP, HEAD_DIM)

    def band(t):
        tb = t.reshape(B, nb, SWA_BLOCK, SWA_KV_HEADS, HEAD_DIM)
        prev = jnp.pad(tb, ((0, 0), (1, 0), (0, 0), (0, 0), (0, 0)))[:, :-1]
        return jnp.concatenate([prev, tb], axis=2)

    kb, vb = band(k), band(v)
    s = jnp.einsum('bnqhgd,bnkhd->bnhgqk', qb, kb).astype(f32) * (HEAD_DIM ** -0.5)
    qpos = jnp.arange(SWA_BLOCK) + SWA_BLOCK
    kpos = jnp.arange(2 * SWA_BLOCK)
    rel = qpos[:, None] - kpos[None, :]
    in_window = (rel >= 0) & (rel < SWA_WINDOW)
    not_pad = (jnp.arange(nb)[:, None] > 0) | (kpos[None, :] >= SWA_BLOCK)
    mask = in_window[None] & not_pad[:, None, :]
    s = jnp.where(mask[None, :, None, None], s, -jnp.inf)
    sink = sinks.astype(f32).reshape(SWA_KV_HEADS, SWA_GROUP)[None, None, :, :, None, None]
    m = jnp.maximum(jnp.max(s, axis=-1, keepdims=True), sink)
    e = jnp.exp(s - m)
    p = e / (jnp.sum(e, axis=-1, keepdims=True) + jnp.exp(sink - m))
    o = jnp.einsum('bnhgqk,bnkhd->bnqhgd', p.astype(v.dtype), vb)
    return o.reshape(B, S, SWA_Q)


def even_mixer(h, w_in, lq1, lk1, lq2, lk2, sub_g, sinks, w_out, lambda_init):
    B, S, _ = h.shape
    proj = h @ w_in
    idx = [int(v) for v in np.cumsum(EVEN_SPLITS)[:-1]]
    qa, ka, va, qs, ks, vs = jnp.split(proj, idx, axis=-1)
    a_out = diff_attention(
        qa.reshape(B, S, 2 * DIFF_HEADS, HEAD_DIM),
        ka.reshape(B, S, 2 * DIFF_HEADS, HEAD_DIM),
        va.reshape(B, S, DIFF_HEADS, DIFF_V_DIM),
        lq1, lk1, lq2, lk2, sub_g, lambda_init)
    b_out = swa_with_sinks(
        qs.reshape(B, S, SWA_Q_HEADS, HEAD_DIM),
        ks.reshape(B, S, SWA_KV_HEADS, HEAD_DIM),
        vs.reshape(B, S, SWA_KV_HEADS, HEAD_DIM),
        sinks)
    return jnp.concatenate([a_out, b_out], axis=-1) @ w_out


def causal_depthwise_conv(x, w, b):
    y = lax.conv_general_dilated(
        x, w[:, None, :], window_strides=(1,), padding=[(CONV_WIDTH - 1, 0)],
        dimension_numbers=('NWC', 'WIO', 'NWC'), feature_group_count=x.shape[-1])
    return y + b


def rglru_mixer(h, w_in, conv_w, conv_b, gate_w, gate_b, lru_lambda, w_out):
    B, S, _ = h.shape
    f32 = jnp.float32
    proj = h @ w_in
    y_branch, x_branch = jnp.split(proj, 2, axis=-1)
    y_branch = jax.nn.gelu(y_branch)
    xc = causal_depthwise_conv(x_branch, conv_w, conv_b)
    xb = xc.reshape(B, S, LRU_BLOCKS, LRU_BLOCK_DIM)
    gates = jnp.einsum('bsnd,gnde->gbsne', xb, gate_w) + gate_b[:, None, None]
    r = jax.nn.sigmoid(gates[0].astype(f32)).reshape(B, S, LRU_WIDTH)
    i = jax.nn.sigmoid(gates[1].astype(f32)).reshape(B, S, LRU_WIDTH)
    log_a = -LRU_C * r * jax.nn.softplus(-lru_lambda.astype(f32))
    a = jnp.exp(log_a)
    u = jnp.sqrt(-jnp.expm1(2.0 * log_a)) * (i * xc.astype(f32))

    def combine(left, right):
        a1, b1 = left
        a2, b2 = right
        return a1 * a2, a2 * b1 + b2

    _, hs = lax.associative_scan(combine, (a, u), axis=1)
    return (hs.astype(h.dtype) * y_branch) @ w_out


def sq_relu_mlp(h, w1, w2):
    z = jax.nn.relu(h @ w1)
    return (z * z) @ w2


def setup_inputs(seed: int = 0) -> dict:
    key = jax.random.key(seed)
    ks = jax.random.split(key, 24)
    f32 = jnp.float32
    nrm = lambda k, shape, s: jax.random.normal(k, shape, f32) * s
    gain = lambda k, shape: 1.0 + 0.05 * jax.random.normal(k, shape, f32)
    a0 = jax.random.uniform(ks[14], (N_ODD, LRU_WIDTH), f32, 0.9, 0.999)
    sig = a0 ** (1.0 / LRU_C)
    return {
        "x": jax.random.normal(ks[0], (BATCH, SEQ, D_MODEL), f32),
        "even_w_in": nrm(ks[1], (N_EVEN, D_MODEL, EVEN_IN), D_MODEL ** -0.5),
        "even_lam_q1": nrm(ks[2], (N_EVEN, HEAD_DIM), 0.1),
        "even_lam_k1": nrm(ks[3], (N_EVEN, HEAD_DIM), 0.1),
        "even_lam_q2": nrm(ks[4], (N_EVEN, HEAD_DIM), 0.1),
        "even_lam_k2": nrm(ks[5], (N_EVEN, HEAD_DIM), 0.1),
        "even_subln_g": gain(ks[6], (N_EVEN, DIFF_V_DIM)),
        "even_sinks": nrm(ks[7], (N_EVEN, SWA_Q_HEADS), 1.0),
        "even_w_out": nrm(ks[8], (N_EVEN, MIX_WIDTH, D_MODEL), MIX_WIDTH ** -0.5),
        "odd_w_in": nrm(ks[9], (N_ODD, D_MODEL, 2 * LRU_WIDTH), D_MODEL ** -0.5),
        "odd_conv_w": nrm(ks[10], (N_ODD, CONV_WIDTH, LRU_WIDTH), CONV_WIDTH ** -0.5),
        "odd_conv_b": nrm(ks[11], (N_ODD, LRU_WIDTH), 0.02),
        "odd_gate_w": nrm(ks[12], (N_ODD, 2, LRU_BLOCKS, LRU_BLOCK_DIM, LRU_BLOCK_DIM), LRU_BLOCK_DIM ** -0.5),
        "odd_gate_b": nrm(ks[13], (N_ODD, 2, LRU_BLOCKS, LRU_BLOCK_DIM), 0.02),
        "odd_lru_lambda": jnp.log(sig) - jnp.log1p(-sig),
        "odd_w_out": nrm(ks[15], (N_ODD, LRU_WIDTH, D_MODEL), LRU_WIDTH ** -0.5),
        "pre_mix_g": gain(ks[16], (DEPTH, D_MODEL)),
        "post_mix_g": gain(ks[17], (DEPTH, D_MODEL)),
        "pre_mlp_g": gain(ks[18], (DEPTH, D_MODEL)),
        "post_mlp_g": gain(ks[19], (DEPTH, D_MODEL)),
        "mlp_w1": nrm(ks[20], (DEPTH, D_MODEL, D_FF), D_MODEL ** -0.5),
        "mlp_w2": nrm(ks[21], (DEPTH, D_FF, D_MODEL), D_FF ** -0.5),
    }


def reference(x, even_w_in, even_lam_q1, even_lam_k1, even_lam_q2, even_lam_k2,
              even_subln_g, even_sinks, even_w_out, odd_w_in, odd_conv_w, odd_conv_b,
              odd_gate_w, odd_gate_b, odd_lru_lambda, odd_w_out, pre_mix_g, post_mix_g,
              pre_mlp_g, post_mlp_g, mlp_w1, mlp_w2):
    for layer in range(DEPTH):
        h = rmsnorm(x, pre_mix_g[layer])
        if layer % 2 == 0:
            e = layer // 2
            lambda_init = 0.8 - 0.6 * math.exp(-0.3 * layer)
            mix = even_mixer(h, even_w_in[e], even_lam_q1[e], even_lam_k1[e],
                             even_lam_q2[e], even_lam_k2[e], even_subln_g[e],
                             even_sinks[e], even_w_out[e], lambda_init)
        else:
            o = layer // 2
            mix = rglru_mixer(h, odd_w_in[o], odd_conv_w[o], odd_conv_b[o],
                              odd_gate_w[o], odd_gate_b[o], odd_lru_lambda[o], odd_w_out[o])
        x = x + rmsnorm(mix, post_mix_g[layer])
        f = sq_relu_mlp(rmsnorm(x, pre_mlp_g[layer]), mlp_w1[layer], mlp_w2[layer])
        x = x + rmsnorm(f, post_mlp_g[layer])
    return x
```

```python
import contextlib
import numpy as np
import ml_dtypes
import concourse.bass as bass
import concourse.mybir as mybir
from concourse.bass_utils import run_bass_kernel_spmd

F32 = mybir.dt.float32
BF16 = mybir.dt.bfloat16
AF = mybir.ActivationFunctionType
ALU = mybir.AluOpType

ENGS = ["pe", "act", "dve", "pool", "sp"]
CH = 30000
CHD = 1900


class Res:
    __slots__ = ("name", "w", "r", "ndma", "dsems")

    def __init__(self, name):
        self.name = name
        self.w = None
        self.r = {}
        self.ndma = 0
        self.dsems = {}


class Prog:
    def __init__(self, nc):
        self.nc = nc
        self.es = contextlib.ExitStack()
        self.q = {e: [] for e in ENGS}
        self.cnt = {e: 0 for e in ENGS}
        self.seen = {e: {} for e in ENGS}
        self.engsem = {}
        self.nsem = 0

    def sem(self, name):
        self.nsem += 1
        return self.es.enter_context(self.nc.semaphore(name))

    def sbuf(self, name, shape, dtype):
        return self.es.enter_context(self.nc.sbuf_tensor(name, list(shape), dtype))

    def psum(self, name, shape, dtype=F32):
        return self.es.enter_context(self.nc.psum_tensor(name, list(shape), dtype))

    def _esem(self, e, c):
        if (e, c) not in self.engsem:
            self.engsem[(e, c)] = self.sem(f"s_{e}_{c}")
        return self.engsem[(e, c)]

    def _dsem(self, res, c):
        if c not in res.dsems:
            res.dsems[c] = self.sem(f"d_{res.name}_{c}")
        return res.dsems[c]

    def _need(self, eng, ev):
        if ev[0] == "e":
            _, e, k = ev
            if e == "pe" and eng == "pe":
                return []
            key = ("e", e)
            if self.seen[eng].get(key, 0) >= k:
                return []
            self.seen[eng][key] = k
            return [(self._esem(e, (k - 1) // CH), (k - 1) % CH + 1)]
        _, res, n = ev
        key = ("d", id(res))
        if self.seen[eng].get(key, 0) >= n:
            return []
        self.seen[eng][key] = n
        return [(self._dsem(res, (n - 1) // CHD), ((n - 1) % CHD + 1) * 16)]

    def op(self, eng, fn, reads=(), writes=(), dma=False, event=True):
        waits = []
        for r in reads:
            if r.w is not None:
                waits += self._need(eng, r.w)
        for w in writes:
            if w.w is not None:
                waits += self._need(eng, w.w)
            for ev in w.r.values():
                waits += self._need(eng, ev)
        inc = None
        if dma:
            tgt = writes[0]
            tgt.ndma += 1
            n = tgt.ndma
            c = (n - 1) // CHD
            if n > 1 and (n - 1) % CHD == 0:
                waits.append((self._dsem(tgt, c - 1), CHD * 16))
            ev = ("d", tgt, n)
            inc = (self._dsem(tgt, c), 16)
        elif event:
            self.cnt[eng] += 1
            k = self.cnt[eng]
            ev = ("e", eng, k)
            inc = (self._esem(eng, (k - 1) // CH), 1)
        else:
            ev = ("e", eng, self.cnt[eng] + 1)
        for r in reads:
            key = (ev[0], ev[1] if ev[0] == "e" else id(ev[1]))
            r.r[key] = ev
        for w in writes:
            w.w = ev
            w.r = {}
        self.q[eng].append((waits, fn, inc))

    def wait_all(self, eng, ress):
        waits = []
        for r in ress:
            if r.w is not None:
                waits += self._need(eng, r.w)
        self.q[eng].append((waits, None, None))

    def check(self):
        val = {}
        pos = {e: 0 for e in ENGS}
        progress = True
        while progress:
            progress = False
            for e in ENGS:
                q = self.q[e]
                while pos[e] < len(q):
                    waits, fn, inc = q[pos[e]]
                    if any(val.get(id(s_), 0) < v for (s_, v) in waits):
                        break
                    if inc is not None:
                        val[id(inc[0])] = val.get(id(inc[0]), 0) + inc[1]
                    pos[e] += 1
                    progress = True
        stuck = {e: pos[e] for e in ENGS if pos[e] < len(self.q[e])}
        if stuck:
            msg = []
            for e, i in stuck.items():
                waits = self.q[e][i][0]
                msg.append(f"{e}@{i}/{len(self.q[e])}: " + ", ".join(f"{s_.name}>={v} (now {val.get(id(s_), 0)})" for s_, v in waits
                                                                   if val.get(id(s_), 0) < v))
            raise RuntimeError("DEADLOCK in recorded program:\n" + "\n".join(msg))
        self.maxsem = max(val.values()) if val else 0

    def emit(self):
        self.check()
        nc = self.nc
        handles = {"pe": nc.tensor, "act": nc.scalar, "dve": nc.vector, "pool": nc.gpsimd, "sp": nc.sync}
        with nc.Block() as block:
            def run(e):
                h = handles[e]
                for waits, fn, inc in self.q[e]:
                    for (s, v) in waits:
                        h.wait_ge(s, v)
                    if fn is not None:
                        ins = fn(h)
                        if inc is not None:
                            ins.then_inc(inc[0], inc[1])

            @block.tensor
            def _(t):
                run("pe")

            @block.scalar
            def _(t):
                run("act")

            @block.vector
            def _(t):
                run("dve")

            @block.gpsimd
            def _(t):
                run("pool")

            @block.sync
            def _(t):
                run("sp")

    def barrier(self):
        for eng in ENGS:
            waits = []
            for e in ["pe", "act", "dve", "pool"]:
                if self.cnt[e] > 0:
                    waits += self._need(eng, ("e", e, self.cnt[e]))
            for r in self.allres:
                if r.ndma > 0:
                    waits += self._need(eng, ("d", r, r.ndma))
            self.q[eng].append((waits, None, None))
        for r in self.allres:
            r.w = None
            r.r = {}

    allres = []

    def res(self, name):
        r = Res(name)
        self.allres.append(r)
        return r

    def ress(self, name, n):
        return [self.res(f"{name}{i}") for i in range(n)]


EPS = 1e-6
LAMBDA_INIT0 = 0.8 - 0.6 * 1.0


class Cfg:
    def __init__(self, D=2048, T=2048, H=8, HQ=16, HKV=2, DFF=8192, NW=6):
        self.D, self.T, self.H, self.HQ, self.HKV, self.DFF, self.NW = D, T, H, HQ, HKV, DFF, NW
        self.TT = 512
        self.KC = D // 128
        self.NT = T // 512
        self.FC = DFF // 128
        self.NB = D // 256
        self.KG2 = min(16, self.FC)
        self.NG2 = self.FC // self.KG2
        self.GRP = HQ // HKV
        assert H * 128 + HQ * 64 == D and self.KC <= 16 and HQ % 2 == 0
        cols = {}
        off = 0
        KC = self.KC
        for nm, n in ([(f"g{l}_{k}", KC) for l in range(2) for k in ("premix", "postmix", "premlp", "postmlp")]
                      + [(f"convw{j}", KC) for j in range(4)] + [("convb", KC), ("gateb0", KC), ("gateb1", KC), ("lrul", KC),
                      ("subg", 1), ("sinks", HQ), ("lq1", 64), ("lk1", 64), ("lq2", 64), ("lk2", 64), ("flag", 1)]):
            cols[nm] = (off, n)
            off += n
        self.cols = cols
        self.NV = off


def build(cfg):
    c = cfg
    D, T, H, HQ, HKV, KC, NT, TT, FC, NB, NW = c.D, c.T, c.H, c.HQ, c.HKV, c.KC, c.NT, c.TT, c.FC, c.NB, c.NW
    T2 = 2 * T
    NCK = T2 // 128
    nc = bass.Bass("TRN2", target_bir_lowering=False)

    def din(name, shape, dtype=F32):
        return nc.dram_tensor(name, list(shape), dtype, kind="ExternalInput")

    def dsc(name, shape, dtype=F32):
        return nc.dram_tensor(name, list(shape), dtype)

    xT = din("xT", [D, T2])
    vecs = din("vecs", [128, c.NV])
    masks = din("masks", [128, 4 * 1024 + 512])
    nA = 2 * H + 2 * HKV + H + HQ // 2
    warrs = {}
    WW = 2048
    woff = [0]

    def warr(name, nblk, kcb):
        warrs[name] = (woff[0] // WW, kcb)
        woff[0] += nblk * 128 * kcb * 128
        return (name, kcb)

    wA = warr("wA", nA, KC)
    wo = [None, None]
    w1 = [None, None]
    w2 = [None, None]
    wo[0] = warr("wo0", KC, KC)
    w1[0] = warr("w1_0", FC, KC)
    w2[0] = warr("w2_0", KC * c.NG2, c.KG2)
    wB = warr("wB", 2 * KC, KC)
    wg = warr("wg", 2 * NB * 2, 2)
    wo[1] = warr("wo1", KC, KC)
    w1[1] = warr("w1_1", FC, KC)
    w2[1] = warr("w2_1", KC * c.NG2, c.KG2)
    warr_order = ["wA", "wo0", "w1_0", "w2_0", "wB", "wg", "wo1", "w1_1", "w2_1"]
    R8 = woff[0] // WW
    c.Ltot = woff[0]
    wall_in = din("wall", [R8, WW])
    wfull = dsc("wfull", [R8, WW], BF16)
    yT = nc.dram_tensor("yT", [D, T], F32, kind="ExternalOutput")
    qT_d = dsc("qT_d", [H * 128, T], BF16)
    kT_d = dsc("kT_d", [H * 128, T2], BF16)
    v_d = dsc("v_d", [T2, H * 128], BF16)
    qsT_d = dsc("qsT_d", [HQ * 64, T], BF16)
    ksT_d = dsc("ksT_d", [HKV * 128, T2], BF16)
    vs_d = dsc("vs_d", [T2, HKV * 128], BF16)
    at_d = dsc("at_d", [D, T], BF16)
    xr_d = dsc("xr_d", [D, T], F32)
    y_d = dsc("y_d", [D, T], BF16)
    xb_d = dsc("xb_d", [D, T], F32)
    a_d = dsc("a_d", [D, T], F32)
    u_d = dsc("u_d", [D, T], F32)
    cc1_in = dsc("cc1_in", [128, KC * 4], F32)
    cc1_out = dsc("cc1_out", [256, KC * 4], F32)
    cc2_in = dsc("cc2_in", [128, 16], F32)
    cc2_out = dsc("cc2_out", [256, 16], F32)

    P = Prog(nc)
    P.allres = []
    R = P.res
    r_qT, r_kT, r_v, r_qsT, r_ksT, r_vs = R("qT"), R("kT"), R("v"), R("qsT"), R("ksT"), R("vs")
    r_at = P.ress("at", NT)
    r_xr = P.ress("xr", NT)
    r_y = P.ress("y", NT)
    r_xb = P.ress("xb", NT)
    r_a = P.ress("a", NT)
    r_u = P.ress("u", NT)
    r_cc1i, r_cc1o, r_cc2i, r_cc2o = R("cc1i"), R("cc1o"), R("cc2i"), R("cc2o")
    r_out = R("out")

    vec = P.sbuf("vec", [128, c.NV], F32); r_vec = R("vec")
    ones = P.sbuf("ones", [128, 128], BF16); r_ones = R("ones")
    onesf = P.sbuf("onesf", [128, 128], BF16); r_onesf = R("onesf")
    mk = P.sbuf("mk", [128, 4 * 1024 + 512], BF16); r_mk = R("mk")
    Wt = [P.sbuf(f"W{i}", [128, 16, 128], BF16) for i in range(NW)]; r_W = P.ress("W", NW)
    rstd = P.sbuf("rstd", [128, TT], F32); r_rstd = R("rstd")
    NSQ, NTF, NSG = 3, 6, 4
    sq = [P.sbuf(f"sq{i}", [128, TT], BF16) for i in range(NSQ)]; r_sq = P.ress("sq", NSQ)
    tf = [P.sbuf(f"tf{i}", [128, TT + 4], F32) for i in range(NTF)]; r_tf = P.ress("tf", NTF)
    sg = [P.sbuf(f"sg{i}", [128, TT], BF16) for i in range(NSG)]; r_sg = P.ress("sg", NSG)
    sm = P.sbuf("sm", [128, 64], F32); r_sm = R("sm")
    cst = P.sbuf("cst", [128, 8 + HQ + KC], F32); r_cst = R("cst")
    hal = P.sbuf("hal", [128, KC, 4], F32); r_hal = R("hal")
    hin = P.sbuf("hin", [128, 16], F32); r_hin = R("hin")
    hfin = P.sbuf("hfin", [128, KC], F32); r_hfin = P.ress("hfin", KC)
    class BV:
        def __init__(self, t, off):
            self.t, self.off = t, off

        def __getitem__(self, idx):
            p_, c_ = idx
            a = (c_.start or 0) + self.off
            b_ = (512 if c_.stop is None else c_.stop) + self.off
            return self.t[p_, a:b_]

    bank2 = [P.psum(f"bk2_{i}", [128, 1024], F32) for i in range(4)]
    bank = [BV(bank2[i // 2], (i % 2) * 512) for i in range(8)]; r_bk = P.ress("bk", 8)

    cnt = {"w": 0, "sq": 0, "tf": 0, "sg": 0, "bk": 0, "ev": 0}

    def col(name, i=0, n=1):
        o, _ = c.cols[name]
        return vec[:, o + i:o + i + n]

    def nxt(kind, n):
        i = cnt[kind] % n
        cnt[kind] += 1
        return i

    def dma(q, out, in_, reads, writes):
        P.op(q, lambda e: e.dma_start(out=out, in_=in_), reads=reads, writes=writes, dma=True)

    def mm(out, lhsT, rhs, start, stop, reads, writes, event=None):
        P.op("pe", lambda e: e.matmul(out, lhsT=lhsT, rhs=rhs, start=start, stop=stop), reads=reads, writes=writes,
             event=stop if event is None else event)

    def act(out, in_, func, reads, writes, bias=None, scale=None, eng="act"):
        kw = {}
        if bias is not None:
            kw["bias"] = bias
        if scale is not None:
            kw["scale"] = scale
        P.op(eng, lambda e: e.activation(out=out, in_=in_, func=func, **kw), reads=reads, writes=writes)

    def tt(eng, out, in0, in1, op, reads, writes):
        P.op(eng, lambda e: e.tensor_tensor(out=out, in0=in0, in1=in1, op=op), reads=reads, writes=writes)

    def ts(eng, out, in0, s1, s2, op0, op1, reads, writes):
        if op1 is None:
            P.op(eng, lambda e: e.tensor_scalar(out=out, in0=in0, scalar1=s1, scalar2=None, op0=op0), reads=reads, writes=writes)
        else:
            P.op(eng, lambda e: e.tensor_scalar(out=out, in0=in0, scalar1=s1, scalar2=s2, op0=op0, op1=op1), reads=reads, writes=writes)

    def stt(eng, out, in0, scalar, in1, op0, op1, reads, writes):
        P.op(eng, lambda e: e.scalar_tensor_tensor(out=out, in0=in0, scalar=scalar, in1=in1, op0=op0, op1=op1), reads=reads, writes=writes)

    def cp(eng, out, in_, reads, writes):
        P.op(eng, lambda e: e.tensor_copy(out=out, in_=in_), reads=reads, writes=writes)

    r_wf = {}

    def wload(src, kcb):
        (name, kcb_), blk = src
        assert kcb_ == kcb
        roff = warrs[name][0] + blk * 8 * kcb
        src_ap = wfull.ap()[roff:roff + 8 * kcb, :].rearrange("a (b k c) -> (a b) k c", k=kcb, c=128)
        i = nxt("w", NW)
        dma("pool", Wt[i][:, 0:kcb, :], src_ap, [r_wf[name]], [r_W[i]])
        return Wt[i], r_W[i]

    evac_toggle = [0]

    def evac_copy(out, in_, reads, writes):
        evac_toggle[0] ^= 1
        if evac_toggle[0]:
            act(out, in_, AF.Copy, reads, writes)
        else:
            cp("dve", out, in_, reads, writes)

    cast_chunks = []
    row_end = {}
    names_sorted = sorted(warrs, key=lambda n_: warrs[n_][0])
    for ai, name in enumerate(names_sorted):
        r0 = warrs[name][0]
        r1 = warrs[names_sorted[ai + 1]][0] if ai + 1 < len(names_sorted) else R8
        r_wf[name] = R(name + "_f")
        for a in range(r0, r1, 128):
            cast_chunks.append((name, a, min(a + 128, r1)))
    cast_pos = [0]

    def cast_some(n=None, upto=None):
        while cast_pos[0] < len(cast_chunks):
            name, a, b_ = cast_chunks[cast_pos[0]]
            if upto is not None and warrs[name][0] > warrs[upto][0]:
                break
            if upto is None and n is not None and n <= 0:
                break
            i = nxt("w", NW)
            nr = b_ - a
            dma("pool", Wt[i][0:nr, :, :], wall_in.ap()[a:b_, :].rearrange("p (k c) -> p k c", c=128), [], [r_W[i]])
            dma("sp", wfull.ap()[a:b_, :].rearrange("p (k c) -> p k c", c=128), Wt[i][0:nr, :, :], [r_W[i]], [r_wf[name]])
            cast_pos[0] += 1
            if n is not None:
                n -= 1

    cast_some(upto="wA")

    dma("sp", vec[:, :], vecs[:, :], [], [r_vec])
    dma("pool", mk[:, :], masks[:, :], [], [r_mk])
    P.op("dve", lambda e: e.memset(ones[:], 1.0), writes=[r_ones])
    ts("dve", onesf[:], ones[:], col("flag"), None, ALU.mult, None, [r_ones, r_vec], [r_onesf])
    o1, _ = c.cols["lq1"]; o2, _ = c.cols["lk1"]; o3, _ = c.cols["lq2"]; o4, _ = c.cols["lk2"]
    tt("dve", sm[:, 0:64], vec[:, o1:o1 + 64], vec[:, o2:o2 + 64], ALU.mult, [r_vec], [r_sm])
    P.op("dve", lambda e: e.reduce_sum(out=cst[:, 4:5], in_=sm[:, 0:64], axis=mybir.AxisListType.X), reads=[r_sm], writes=[r_cst])
    tt("dve", sm[:, 0:64], vec[:, o3:o3 + 64], vec[:, o4:o4 + 64], ALU.mult, [r_vec, r_cst], [r_sm])
    P.op("dve", lambda e: e.reduce_sum(out=cst[:, 5:6], in_=sm[:, 0:64], axis=mybir.AxisListType.X), reads=[r_sm], writes=[r_cst])
    act(cst[:, 4:6], cst[:, 4:6], AF.Exp, [r_cst], [r_cst])
    tt("dve", cst[:, 6:7], cst[:, 5:6], cst[:, 4:5], ALU.subtract, [r_cst], [r_cst])
    ts("dve", cst[:, 0:1], cst[:, 6:7], -LAMBDA_INIT0, None, ALU.add, None, [r_cst], [r_cst])
    ts("dve", cst[:, 1:2], col("subg"), 1.0 - LAMBDA_INIT0, None, ALU.mult, None, [r_vec, r_cst], [r_cst])
    ES = 8
    so, _ = c.cols["sinks"]
    act(cst[:, ES:ES + HQ], vec[:, so:so + HQ], AF.Exp, [r_vec, r_cst], [r_cst])
    C1 = ES + HQ
    lo, _ = c.cols["lrul"]
    act(sm[:, 0:KC], vec[:, lo:lo + KC], AF.Exp, [r_vec], [r_sm], scale=-1.0)
    act(sm[:, 16:16 + KC], sm[:, 0:KC], AF.Ln, [r_sm], [r_sm], bias=1.0)
    ts("dve", sm[:, 32:32 + KC], sm[:, 0:KC], 1.0, None, ALU.min, None, [r_sm], [r_sm])
    ts("dve", sm[:, 48:48 + KC], sm[:, 32:32 + KC], -0.25, 1.0 / 3.0, ALU.mult, ALU.add, [r_sm], [r_sm])
    tt("dve", sm[:, 48:48 + KC], sm[:, 48:48 + KC], sm[:, 32:32 + KC], ALU.mult, [r_sm], [r_sm])
    ts("dve", sm[:, 48:48 + KC], sm[:, 48:48 + KC], -0.5, None, ALU.add, None, [r_sm], [r_sm])
    tt("dve", sm[:, 48:48 + KC], sm[:, 48:48 + KC], sm[:, 32:32 + KC], ALU.mult, [r_sm], [r_sm])
    ts("dve", sm[:, 48:48 + KC], sm[:, 48:48 + KC], 1.0, None, ALU.add, None, [r_sm], [r_sm])
    tt("dve", sm[:, 48:48 + KC], sm[:, 48:48 + KC], sm[:, 32:32 + KC], ALU.mult, [r_sm], [r_sm])
    ts("dve", sm[:, 32:32 + KC], sm[:, 0:KC], 0.05, None, ALU.is_lt, None, [r_sm], [r_sm])
    tt("dve", sm[:, 48:48 + KC], sm[:, 48:48 + KC], sm[:, 16:16 + KC], ALU.subtract, [r_sm], [r_sm])
    tt("dve", sm[:, 48:48 + KC], sm[:, 48:48 + KC], sm[:, 32:32 + KC], ALU.mult, [r_sm], [r_sm])
    tt("dve", sm[:, 48:48 + KC], sm[:, 48:48 + KC], sm[:, 16:16 + KC], ALU.add, [r_sm], [r_sm])
    ts("dve", cst[:, C1:C1 + KC], sm[:, 48:48 + KC], -8.0, None, ALU.mult, None, [r_sm, r_cst], [r_cst])

    def rms_stats(src_chunk, src_reads, n_chunks, dim):
        b = 7
        for kc in range(n_chunks):
            i = nxt("sq", NSQ)
            act(sq[i][:, :], src_chunk(kc), AF.Square, src_reads(kc), [r_sq[i]])
            mm(bank[b][:, :], ones[:, :], sq[i][:, :], kc == 0, kc == n_chunks - 1, [r_ones, r_sq[i]], [r_bk[b]], event=True)
        ts("dve", rstd[:, :], bank[b][:, :], 1.0 / dim, EPS, ALU.mult, ALU.add, [r_bk[b]], [r_rstd])
        act(rstd[:, :], rstd[:, :], AF.Sqrt, [r_rstd], [r_rstd])
        P.op("dve", lambda e: e.reciprocal(out=rstd[:, :], in_=rstd[:, :]), reads=[r_rstd], writes=[r_rstd])

    def norm_to_bf16(Xt, r_X, gname, HBt, r_HB):
        rms_stats(lambda kc: Xt[:, kc, :], lambda kc: [r_X[kc]], KC, D)
        for kc in range(KC):
            stt("dve", HBt[:, kc, :], Xt[:, kc, :], col(gname, kc), rstd[:, :], ALU.mult, ALU.mult,
                [r_X[kc], r_vec, r_rstd], [r_HB[kc]])

    def proj_fm(HBt, r_HB, wsrc, kcb, k0, start, stop, b):
        Wb, rW = wload(wsrc, kcb)
        for kc in range(kcb):
            mm(bank[b][:, :], Wb[:, kc, :], HBt[:, k0 + kc, :], start and kc == 0, stop and kc == kcb - 1,
               [rW, r_HB[k0 + kc]], [r_bk[b]], event=(kc == kcb - 1))

    def add_norm_residual(Xt, r_X, Ft, r_F, gname):
        rms_stats(lambda kc: Ft[:, kc, :], lambda kc: [r_F[kc]], KC, D)
        for kc in range(KC):
            stt("dve", Ft[:, kc, :], Ft[:, kc, :], col(gname, kc), rstd[:, :], ALU.mult, ALU.mult,
                [r_F[kc], r_vec, r_rstd], [r_F[kc]])
            tt("dve", Xt[:, kc, :], Xt[:, kc, :], Ft[:, kc, :], ALU.add, [r_X[kc], r_F[kc]], [r_X[kc]])

    def mixout_mlp(l, Xt, r_X, Ft, r_F, HBt, r_HB, Zt, r_Z):
        for m in range(KC):
            b = nxt("bk", 6)
            proj_fm(HBt, r_HB, (wo[l], m), KC, 0, True, True, b)
            evac_copy(Ft[:, m, :], bank[b][:, :], [r_bk[b]], [r_F[m]])
        add_norm_residual(Xt, r_X, Ft, r_F, f"g{l}_postmix")
        norm_to_bf16(Xt, r_X, f"g{l}_premlp", HBt, r_HB)
        for m in range(FC):
            b = nxt("bk", 6)
            proj_fm(HBt, r_HB, (w1[l], m), KC, 0, True, True, b)
            i = nxt("tf", NTF)
            act(tf[i][:, 0:TT], bank[b][:, :], AF.Relu, [r_bk[b]], [r_tf[i]])
            tt("dve", Zt[:, m, :], tf[i][:, 0:TT], tf[i][:, 0:TT], ALU.mult, [r_tf[i]], [r_Z[m]])
        for m in range(KC):
            b = nxt("bk", 6)
            for g in range(c.NG2):
                proj_fm(Zt, r_Z, (w2[l], m * c.NG2 + g), c.KG2, g * c.KG2, g == 0, g == c.NG2 - 1, b)
            evac_copy(Ft[:, m, :], bank[b][:, :], [r_bk[b]], [r_F[m]])
        add_norm_residual(Xt, r_X, Ft, r_F, f"g{l}_postmlp")

    xT3 = xT.ap().rearrange("(kc p) n -> p kc n", p=128)

    with contextlib.ExitStack() as ph:
        X = ph.enter_context(nc.sbuf_tensor("X_a", [128, KC, TT], F32)); r_X = P.ress("Xa", KC)
        HB = ph.enter_context(nc.sbuf_tensor("HB_a", [128, KC, TT], BF16)); r_HB = P.ress("HBa", KC)
        v3 = v_d.ap().rearrange("(c p) n -> p c n", p=128)
        vs3 = vs_d.ap().rearrange("(c p) n -> p c n", p=128)
        for t in range(2 * NT):
            own = t >= NT
            tsl = slice(t * TT, (t + 1) * TT)
            dma("sp", X[:, :, :], xT3[:, :, tsl], [], r_X)
            norm_to_bf16(X, r_X, "g0_premix", HB, r_HB)

            def fm_block(widx, dst, r_dst, row0, csl):
                b = nxt("bk", 6)
                proj_fm(HB, r_HB, (wA, widx), KC, 0, True, True, b)
                i = nxt("sg", NSG)
                evac_copy(sg[i][:, :], bank[b][:, :], [r_bk[b]], [r_sg[i]])
                dma("sp", dst[row0:row0 + 128, csl], sg[i][:, :], [r_sg[i]], [r_dst])

            def tm_block(widx, dst3, r_dst, col0):
                b = nxt("bk", 6)
                Wb, rW = wload((wA, widx), KC)
                for cc in range(4):
                    for kc in range(KC):
                        mm(bank[b][:, cc * 128:(cc + 1) * 128], HB[:, kc, cc * 128:(cc + 1) * 128], Wb[:, kc, :],
                           kc == 0, kc == KC - 1, [rW, r_HB[kc]], [r_bk[b]])
                i = nxt("sg", NSG)
                evac_copy(sg[i][:, :], bank[b][:, :], [r_bk[b]], [r_sg[i]])
                dma("sp", dst3[:, t * 4:(t + 1) * 4, col0:col0 + 128], sg[i][:, :].rearrange("p (c n) -> p c n", n=128),
                    [r_sg[i]], [r_dst])

            for h in range(H):
                fm_block(h, kT_d, r_kT, h * 128, tsl)
            for h in range(H):
                tm_block(H + h, v3, r_v, h * 128)
            for g in range(HKV):
                fm_block(2 * H + g, ksT_d, r_ksT, g * 128, tsl)
            for g in range(HKV):
                tm_block(2 * H + HKV + g, vs3, r_vs, g * 128)
            if own:
                osl = slice((t - NT) * TT, (t - NT + 1) * TT)
                for h in range(H):
                    fm_block(2 * H + 2 * HKV + h, qT_d, r_qT, h * 128, osl)
                for j in range(HQ // 2):
                    fm_block(2 * H + 2 * HKV + H + j, qsT_d, r_qsT, j * 128, osl)
            n_l0 = sum(1 for (nm_, _a, _b) in cast_chunks if nm_ in ("wo0", "w1_0", "w2_0"))
            cast_some(n=-(-n_l0 // (2 * NT)))
        cast_some(upto="w2_0")
        P.barrier()

    with contextlib.ExitStack() as ph:
        kTb = [ph.enter_context(nc.sbuf_tensor(f"kTb{i}", [128, T2], BF16)) for i in range(2)]; r_kTb = P.ress("kTb", 2)
        vhb = [ph.enter_context(nc.sbuf_tensor(f"vhb{i}", [128, NCK, 128], BF16)) for i in range(2)]; r_vhb = P.ress("vhb", 2)
        qTb = [ph.enter_context(nc.sbuf_tensor(f"qTb{i}", [128, T], BF16)) for i in range(2)]; r_qTb = P.ress("qTb", 2)
        NPT = 4
        pT = [ph.enter_context(nc.sbuf_tensor(f"pT{i}", [128, 2 * TT], BF16)) for i in range(NPT)]; r_pT = P.ress("pT", NPT)
        cpt = [0]
        SC = 0.125
        pacc = [ph.enter_context(nc.sbuf_tensor(f"pacc{k_}", [128, 2 * TT], F32)) for k_ in range(2)]
        r_pacc = [R(f"pacc{k_}") for k_ in range(2)]
        phl = [ph.enter_context(nc.sbuf_tensor(f"phl{i_}", [128, TT], BF16)) for i_ in range(2)]
        r_phl = P.ress("phl", 2)
        def head_loads(h_):
            s_ = h_ % 2
            dma("sp", kTb[s_][:, :], kT_d[h_ * 128:(h_ + 1) * 128, :], [r_kT], [r_kTb[s_]])
            dma("sp", vhb[s_][:, :, :], v3[:, :, h_ * 128:(h_ + 1) * 128], [r_v], [r_vhb[s_]])
            dma("sp", qTb[s_][:, :], qT_d[h_ * 128:(h_ + 1) * 128, :], [r_qT], [r_qTb[s_]])

        head_loads(0)
        for h in range(H):
            if h + 1 < H:
                head_loads(h + 1)
            cast_some(n=-(-(len(cast_chunks) - cast_pos[0]) // (H - h)))
            s = h % 2
            for tq in range(NT):
                nk = (T + (tq + 1) * TT) // 128
                kd0 = (T + tq * TT) // 128
                qsl = slice(tq * TT, (tq + 1) * TT)

                def qk(kc):
                    for j in range(2):
                        b = (kc % 3) * 2 + j
                        mm(bank[b][:, :], kTb[s][j * 64:(j + 1) * 64, kc * 128:(kc + 1) * 128], qTb[s][j * 64:(j + 1) * 64, qsl],
                           True, True, [r_kTb[s], r_qTb[s]], [r_bk[b]])

                def pv(kc):
                    sc = kc % 3
                    ip = cpt[0] % NPT
                    cpt[0] += 1
                    act(pT[ip][:, :], bank2[sc][:, :], AF.Exp, [r_bk[2 * sc], r_bk[2 * sc + 1]], [r_pT[ip]], scale=SC)
                    if kc >= kd0:
                        o = kc - kd0
                        tt("dve", pT[ip][:, :], pT[ip][:, :], mk[:, o * 1024:(o + 1) * 1024], ALU.mult, [r_pT[ip], r_mk], [r_pT[ip]])
                    last = kc == nk - 1
                    for j in range(2):
                        mm(bank[6 + j][:, :], vhb[s][:, kc, :], pT[ip][:, j * TT:(j + 1) * TT], kc == 0, last, [r_vhb[s], r_pT[ip]], [r_bk[6 + j]],
                           event=(j == 1))
                    kind = 0 if kc < T // 128 else 1
                    tt("dve", pacc[kind][:, :], pacc[kind][:, :], pT[ip][:, :], ALU.add, [r_pacc[kind], r_pT[ip]], [r_pacc[kind]])

                for k_ in range(2):
                    P.op("dve", (lambda t_: (lambda e: e.memset(t_, 0.0)))(pacc[k_][:, :]), writes=[r_pacc[k_]])
                qk(0)
                qk(1)
                for kc in range(nk):
                    if kc + 2 < nk:
                        qk(kc + 2)
                    pv(kc)
                for j in range(2):
                    lb = j
                    n_mm = 0
                    for kind in range(2):
                        on = onesf if kind == 0 else ones
                        cp("dve", phl[0][:, :], pacc[kind][:, j * TT:(j + 1) * TT], [r_pacc[kind]], [r_phl[0]])
                        tt("dve", phl[1][:, :], pacc[kind][:, j * TT:(j + 1) * TT], phl[0][:, :], ALU.subtract, [r_pacc[kind], r_phl[0]], [r_phl[1]])
                        for hl in range(2):
                            mm(bank[lb][:, :], on[:, :], phl[hl][:, :], n_mm == 0, n_mm == 3, [r_ones, r_onesf, r_phl[hl]], [r_bk[lb]], event=True)
                            n_mm += 1
                f = [nxt("tf", NTF) for _ in range(4)]
                for j in range(2):
                    P.op("dve", (lambda jj, ff: (lambda e: e.reciprocal(out=tf[ff][:, 0:TT], in_=bank[jj][:, :])))(j, f[j]),
                         reads=[r_bk[j]], writes=[r_tf[f[j]]])
                    tt("dve", tf[f[j]][:, 0:TT], bank[6 + j][:, :], tf[f[j]][:, 0:TT], ALU.mult, [r_bk[6 + j], r_tf[f[j]]], [r_tf[f[j]]])
                stt("dve", tf[f[2]][:, 0:TT], tf[f[1]][:, 0:TT], cst[:, 0:1], tf[f[0]][:, 0:TT], ALU.mult, ALU.add,
                    [r_tf[f[0]], r_tf[f[1]], r_cst], [r_tf[f[2]]])
                rms_stats(lambda kc: tf[f[2]][:, 0:TT], lambda kc: [r_tf[f[2]]], 1, 128)
                i = nxt("sg", NSG)
                stt("dve", sg[i][:, :], tf[f[2]][:, 0:TT], cst[:, 1:2], rstd[:, :], ALU.mult, ALU.mult,
                    [r_tf[f[2]], r_cst, r_rstd], [r_sg[i]])
                dma("sp", at_d[h * 128:(h + 1) * 128, qsl], sg[i][:, :], [r_sg[i]], [r_at[tq]])

        mks = mk[:, 4096:4608]
        for bq in range(HQ // 2):
            g = (2 * bq) // c.GRP
            s = bq % 2
            dma("sp", kTb[s][:, :], ksT_d[g * 128:(g + 1) * 128, :], [r_ksT], [r_kTb[s]])
            dma("sp", vhb[s][:, :, :], vs3[:, :, g * 128:(g + 1) * 128], [r_vs], [r_vhb[s]])
            dma("sp", qTb[s][:, :], qsT_d[bq * 128:(bq + 1) * 128, :], [r_qsT], [r_qTb[s]])
            for tq in range(NT):
                qsl = slice(tq * TT, (tq + 1) * TT)
                isg = nxt("sg", NSG)
                for hh in range(2):
                    ps = slice(hh * 64, (hh + 1) * 64)
                    hq = 2 * bq + hh
                    for half in range(2):
                        b = half
                        for qq in range(2):
                            i = tq * 4 + half * 2 + qq
                            ci = T // 128 + i
                            for w_, kc in enumerate((ci - 1, ci)):
                                cs = slice((qq * 2 + w_) * 128, (qq * 2 + w_ + 1) * 128)
                                mm(bank[b][:, cs], kTb[s][ps, kc * 128:(kc + 1) * 128], qTb[s][ps, i * 128:(i + 1) * 128],
                                   True, True, [r_kTb[s], r_qTb[s]], [r_bk[b]], event=(qq == 1 and w_ == 1))
                        ip = cpt[0] % NPT
                        cpt[0] += 1
                        act(pT[ip][:, 0:TT], bank[b][:, :], AF.Exp, [r_bk[b]], [r_pT[ip]], scale=SC)
                        tt("dve", pT[ip][:, 0:TT], pT[ip][:, 0:TT], mks, ALU.mult, [r_pT[ip], r_mk], [r_pT[ip]])
                        for qq in range(2):
                            i = tq * 4 + half * 2 + qq
                            ci = T // 128 + i
                            osl = slice((half * 2 + qq) * 128, (half * 2 + qq + 1) * 128)
                            for w_, kc in enumerate((ci - 1, ci)):
                                cs = slice((qq * 2 + w_) * 128, (qq * 2 + w_ + 1) * 128)
                                mm(bank[4][:, osl], vhb[s][:, kc, :], pT[ip][:, cs], w_ == 0, w_ == 1, [r_vhb[s], r_pT[ip]], [r_bk[4]], event=False)
                                on = onesf if kc < T // 128 else ones
                                mm(bank[6][:, osl], on[:, :], pT[ip][:, cs], w_ == 0, w_ == 1, [r_ones, r_onesf, r_pT[ip]], [r_bk[6], r_bk[4]],
                                   event=(w_ == 1))
                    f0 = nxt("tf", NTF)
                    ts("dve", tf[f0][ps, 0:TT], bank[6][ps, :], cst[ps, ES + hq:ES + hq + 1], None, ALU.add, None, [r_bk[6], r_cst], [r_tf[f0]])
                    P.op("dve", (lambda ff, pp: (lambda e: e.reciprocal(out=tf[ff][pp, 0:TT], in_=tf[ff][pp, 0:TT])))(f0, ps),
                         reads=[r_tf[f0]], writes=[r_tf[f0]])
                    tt("dve", sg[isg][ps, :], bank[4][ps, :], tf[f0][ps, 0:TT], ALU.mult, [r_bk[4], r_tf[f0]], [r_sg[isg]])
                dma("sp", at_d[H * 128 + bq * 128:H * 128 + (bq + 1) * 128, qsl], sg[isg][:, :], [r_sg[isg]], [r_at[tq]])
        P.barrier()

    at3 = at_d.ap().rearrange("(kc p) n -> p kc n", p=128)
    xr3 = xr_d.ap().rearrange("(kc p) n -> p kc n", p=128)
    xb3 = xb_d.ap().rearrange("(kc p) n -> p kc n", p=128)
    yT3 = yT.ap().rearrange("(kc p) n -> p kc n", p=128)
    PAIRS = [[2 * i, 2 * i + 1] for i in range(4)]
    with contextlib.ExitStack() as ph:
        X = ph.enter_context(nc.sbuf_tensor("X_c", [128, KC, TT], F32)); r_X = P.ress("Xc", KC)
        Fb = ph.enter_context(nc.sbuf_tensor("F_c", [128, KC, TT], F32)); r_F = P.ress("Fc", KC)
        HB = ph.enter_context(nc.sbuf_tensor("HB_c", [128, KC, TT], BF16)); r_HB = P.ress("HBc", KC)
        Z = ph.enter_context(nc.sbuf_tensor("Z_c", [128, FC, TT], BF16)); r_Z = P.ress("Zc", FC)
        for t in range(NT):
            tsl = slice(t * TT, (t + 1) * TT)
            dma("sp", X[:, :, :], xT3[:, :, T + t * TT:T + (t + 1) * TT], [], r_X)
            dma("sp", HB[:, :, :], at3[:, :, tsl], [r_at[t]], r_HB)
            mixout_mlp(0, X, r_X, Fb, r_F, HB, r_HB, Z, r_Z)
            dma("sp", xr3[:, :, tsl], X[:, :, :], r_X, [r_xr[t]])
            norm_to_bf16(X, r_X, "g1_premix", HB, r_HB)
            for m in range(KC):
                b = nxt("bk", 6)
                proj_fm(HB, r_HB, (wB, m), KC, 0, True, True, b)
                i = nxt("sg", NSG)
                act(sg[i][:, :], bank[b][:, :], AF.Gelu_apprx_tanh, [r_bk[b]], [r_sg[i]])
                dma("sp", y_d[m * 128:(m + 1) * 128, tsl], sg[i][:, :], [r_sg[i]], [r_y[t]])
            for m in range(KC):
                b = nxt("bk", 6)
                proj_fm(HB, r_HB, (wB, KC + m), KC, 0, True, True, b)
                i = nxt("tf", NTF)
                evac_copy(tf[i][:, 0:TT], bank[b][:, :], [r_bk[b]], [r_tf[i]])
                dma("sp", xb_d[m * 128:(m + 1) * 128, tsl], tf[i][:, 0:TT], [r_tf[i]], [r_xb[t]])
                if t == NT - 1:
                    dma("sp", cc1_in[:, m * 4:m * 4 + 3], tf[i][:, TT - 3:TT], [r_tf[i]], [r_cc1i])
        P.barrier()

    P.op("pool", lambda e: e.collective_compute("AllGather", ALU.bypass, replica_groups=PAIRS,
                                                ins=[cc1_in.ap().opt()], outs=[cc1_out.ap().opt()]),
         reads=[r_cc1i], writes=[r_cc1o])
    dma("sp", hal[:, :, :], cc1_out.ap()[0:128, :].rearrange("p (k c) -> p k c", c=4), [r_cc1o], [r_hal])
    ts("dve", hal[:, :, :], hal[:, :, :], col("flag"), None, ALU.mult, None, [r_hal, r_vec], [r_hal])

    with contextlib.ExitStack() as ph:
        XB = ph.enter_context(nc.sbuf_tensor("XB_e", [128, KC, TT + 4], F32)); r_XB = P.ress("XBe", KC)
        XC = ph.enter_context(nc.sbuf_tensor("XC_e", [128, KC, TT], F32)); r_XC = P.ress("XCe", KC)
        XCB = ph.enter_context(nc.sbuf_tensor("XCB_e", [128, KC, TT], BF16)); r_XCB = P.ress("XCBe", KC)
        Rg = [ph.enter_context(nc.sbuf_tensor(f"Rg{i}_e", [128, KC, TT], F32)) for i in range(2)]
        r_Rg = [P.ress(f"Rg{i}e", KC) for i in range(2)]
        for t in range(NT):
            tsl = slice(t * TT, (t + 1) * TT)
            if t == 0:
                cp("pool", XB[:, :, 0:3], hal[:, :, 0:3], [r_hal], r_XB)
            else:
                cp("pool", XB[:, :, 0:3], XB[:, :, TT:TT + 3], r_XB, r_XB)
            dma("sp", XB[:, :, 3:3 + TT], xb3[:, :, tsl], [r_xb[t]], r_XB)
            for cc in range(KC):
                act(XC[:, cc, :], XB[:, cc, 3:3 + TT], AF.Identity, [r_XB[cc], r_vec], [r_XC[cc]],
                    bias=col("convb", cc), scale=col("convw3", cc))
                for j in range(3):
                    stt("dve", XC[:, cc, :], XB[:, cc, j:j + TT], col(f"convw{j}", cc), XC[:, cc, :], ALU.mult, ALU.add,
                        [r_XB[cc], r_XC[cc], r_vec], [r_XC[cc]])
                cp("pool", XCB[:, cc, :], XC[:, cc, :], [r_XC[cc]], [r_XCB[cc]])
            for n in range(NB):
                for m in range(2):
                    co = 2 * n + m
                    for gi in range(2):
                        b = nxt("bk", 6)
                        Wb, rW = wload((wg, (gi * NB + n) * 2 + m), 2)
                        for kc in range(2):
                            mm(bank[b][:, :], Wb[:, kc, :], XCB[:, 2 * n + kc, :], kc == 0, kc == 1, [rW, r_XCB[2 * n + kc]], [r_bk[b]])
                        act(Rg[gi][:, co, :], bank[b][:, :], AF.Sigmoid, [r_bk[b], r_vec], [r_Rg[gi][co]], bias=col(f"gateb{gi}", co))
            for co in range(KC):
                act(Rg[0][:, co, :], Rg[0][:, co, :], AF.Exp, [r_Rg[0][co], r_cst], [r_Rg[0][co]], scale=cst[:, C1 + co:C1 + co + 1])
            for co in range(KC):
                it, iu, is_ = [nxt("tf", NTF) for _ in range(3)]
                A_, T_, U_, S_ = Rg[0][:, co, :], tf[it][:, 0:TT], tf[iu][:, 0:TT], tf[is_][:, 0:TT]
                dma("sp", a_d[co * 128:(co + 1) * 128, tsl], A_, [r_Rg[0][co]], [r_a[t]])
                tt("dve", T_, A_, A_, ALU.mult, [r_Rg[0][co]], [r_tf[it]])
                act(T_, T_, AF.Sqrt, [r_tf[it]], [r_tf[it]], bias=1.0, scale=-1.0)
                tt("pool", Rg[1][:, co, :], Rg[1][:, co, :], XC[:, co, :], ALU.mult, [r_Rg[1][co], r_XC[co]], [r_Rg[1][co]])
                tt("dve", U_, T_, Rg[1][:, co, :], ALU.mult, [r_tf[it], r_Rg[1][co]], [r_tf[iu]])
                dma("sp", u_d[co * 128:(co + 1) * 128, tsl], U_, [r_tf[iu]], [r_u[t]])
                init = 0.0 if t == 0 else hfin[:, co:co + 1]
                P.op("dve", (lambda o_, a_, u_, i_: (lambda e: e.tensor_tensor_scan(out=o_, data0=a_, data1=u_, initial=i_,
                                                                                    op0=ALU.mult, op1=ALU.add)))(S_, A_, U_, init),
                     reads=[r_Rg[0][co], r_tf[iu], r_hfin[co]], writes=[r_tf[is_]])
                cp("dve", hfin[:, co:co + 1], tf[is_][:, TT - 1:TT], [r_tf[is_]], [r_hfin[co]])
        dma("sp", cc2_in[:, 0:KC], hfin[:, :], r_hfin, [r_cc2i])
        P.barrier()

    P.op("pool", lambda e: e.collective_compute("AllGather", ALU.bypass, replica_groups=PAIRS,
                                                ins=[cc2_in.ap().opt()], outs=[cc2_out.ap().opt()]),
         reads=[r_cc2i], writes=[r_cc2o])
    dma("sp", hin[:, 0:KC], cc2_out[0:128, 0:KC], [r_cc2o], [r_hin])
    ts("dve", hin[:, 0:KC], hin[:, 0:KC], col("flag"), None, ALU.mult, None, [r_hin, r_vec], [r_hin])

    with contextlib.ExitStack() as ph:
        X = ph.enter_context(nc.sbuf_tensor("X_f", [128, KC, TT], F32)); r_X = P.ress("Xf", KC)
        Fb = ph.enter_context(nc.sbuf_tensor("F_f", [128, KC, TT], F32)); r_F = P.ress("Ff", KC)
        HB = ph.enter_context(nc.sbuf_tensor("HB_f", [128, KC, TT], BF16)); r_HB = P.ress("HBf", KC)
        Z = ph.enter_context(nc.sbuf_tensor("Z_f", [128, FC, TT], BF16)); r_Z = P.ress("Zf", FC)
        for t in range(NT):
            tsl = slice(t * TT, (t + 1) * TT)
            dma("sp", X[:, :, :], xr3[:, :, tsl], [r_xr[t]], r_X)
            for co in range(KC):
                ia, iu, is_ = [nxt("tf", NTF) for _ in range(3)]
                iy = nxt("sg", NSG)
                A_, U_, S_ = tf[ia][:, 0:TT], tf[iu][:, 0:TT], tf[is_][:, 0:TT]
                dma("sp", A_, a_d[co * 128:(co + 1) * 128, tsl], [r_a[t]], [r_tf[ia]])
                dma("sp", U_, u_d[co * 128:(co + 1) * 128, tsl], [r_u[t]], [r_tf[iu]])
                dma("sp", sg[iy][:, :], y_d[co * 128:(co + 1) * 128, tsl], [r_y[t]], [r_sg[iy]])
                init = hin[:, co:co + 1] if t == 0 else hfin[:, co:co + 1]
                P.op("dve", (lambda o_, a_, u_, i_: (lambda e: e.tensor_tensor_scan(out=o_, data0=a_, data1=u_, initial=i_,
                                                                                    op0=ALU.mult, op1=ALU.add)))(S_, A_, U_, init),
                     reads=[r_tf[ia], r_tf[iu], r_hfin[co], r_hin], writes=[r_tf[is_]])
                cp("dve", hfin[:, co:co + 1], tf[is_][:, TT - 1:TT], [r_tf[is_]], [r_hfin[co]])
                tt("dve", HB[:, co, :], S_, sg[iy][:, :], ALU.mult, [r_tf[is_], r_sg[iy]], [r_HB[co]])
            mixout_mlp(1, X, r_X, Fb, r_F, HB, r_HB, Z, r_Z)
            dma("sp", yT3[:, :, tsl], X[:, :, :], r_X, [r_out])
    P.wait_all("sp", [r_out])
    P.emit()
    P.es.close()
    return nc


def blockify(W, kcb):
    Kin, N = W.shape
    G, J = Kin // (128 * kcb), N // 128
    return np.ascontiguousarray(W.reshape(G, kcb, 128, J, 128).transpose(3, 0, 2, 1, 4).reshape(J * G, 128, kcb, 128))


def fm(v, KC):
    return np.asarray(v, np.float32).reshape(KC, 128).T


def prep_shared(cfg, inp):
    c = cfg
    D, H, HQ, HKV, KC = c.D, c.H, c.HQ, c.HKV, c.KC
    f = lambda a: np.asarray(a, np.float32)
    w_in = f(inp["even_w_in"])[0]
    QK, DV, SQ, SKV = H * 128, H * 128, HQ * 64, HKV * 64
    o = np.cumsum([0, QK, QK, DV, SQ, SKV, SKV])
    qa, ka, va, qs, ks, vs = [w_in[:, o[i]:o[i + 1]] for i in range(6)]
    blocks = [ka, va]
    blocks += [np.concatenate([ks[:, g * 64:(g + 1) * 64]] * 2, 1) for g in range(HKV)]
    blocks += [np.concatenate([vs[:, g * 64:(g + 1) * 64]] * 2, 1) for g in range(HKV)]
    blocks += [qa, qs]
    sh = {"wA": blockify(np.concatenate(blocks, 1), KC)}
    sh["wo0"] = blockify(f(inp["even_w_out"])[0], KC)
    sh["wo1"] = blockify(f(inp["odd_w_out"])[0], KC)
    for l in range(2):
        sh[f"w1_{l}"] = blockify(f(inp["mlp_w1"])[l], KC)
        sh[f"w2_{l}"] = blockify(f(inp["mlp_w2"])[l], c.KG2)
    sh["wB"] = blockify(f(inp["odd_w_in"])[0], KC)
    gw = f(inp["odd_gate_w"])[0]
    sh["wg"] = np.concatenate([blockify(gw[gi, n], 2) for gi in range(2) for n in range(c.NB)], 0)
    vec = np.zeros((128, c.NV), np.float32)

    def put(name, arr):
        o_, n_ = c.cols[name]
        vec[:, o_:o_ + n_] = arr
    for l in range(2):
        put(f"g{l}_premix", fm(f(inp["pre_mix_g"])[l], KC))
        put(f"g{l}_postmix", fm(f(inp["post_mix_g"])[l], KC))
        put(f"g{l}_premlp", fm(f(inp["pre_mlp_g"])[l], KC))
        put(f"g{l}_postmlp", fm(f(inp["post_mlp_g"])[l], KC))
    for j in range(4):
        put(f"convw{j}", fm(f(inp["odd_conv_w"])[0, j], KC))
    put("convb", fm(f(inp["odd_conv_b"])[0], KC))
    gb = f(inp["odd_gate_b"])[0]
    put("gateb0", fm(gb[0].reshape(-1), KC))
    put("gateb1", fm(gb[1].reshape(-1), KC))
    put("lrul", fm(f(inp["odd_lru_lambda"])[0], KC))
    put("subg", f(inp["even_subln_g"])[0].reshape(128, 1))
    put("sinks", np.broadcast_to(f(inp["even_sinks"])[0][None, :], (128, HQ)))
    for nm, key in (("lq1", "even_lam_q1"), ("lk1", "even_lam_k1"), ("lq2", "even_lam_q2"), ("lk2", "even_lam_k2")):
        put(nm, np.broadcast_to(f(inp[key])[0][None, :], (128, 64)))
    k = np.arange(128)[:, None]
    q = np.arange(512)[None, :]
    m = []
    for o_ in range(4):
        m += [(o_ * 128 + k <= q), (o_ * 128 + k <= q)]
    q1 = np.arange(128)[None, :]
    prev, cur = (k > q1), (k <= q1)
    m.append(np.concatenate([prev, cur, prev, cur], 1))
    sh["masks"] = np.concatenate(m, 1).astype(np.float32)
    return sh, vec


def run(cfg, inp, nc=None):
    c = cfg
    x = np.asarray(inp["x"], np.float32)
    B, S, D = x.shape
    T = c.T
    assert S == 2 * T and B == 4 and D == c.D
    sh, vec = prep_shared(c, inp)
    in_maps = []
    order = ["wA", "wo0", "w1_0", "w2_0", "wB", "wg", "wo1", "w1_1", "w2_1"]
    wall = np.ascontiguousarray(np.concatenate([sh[k_].reshape(-1) for k_ in order]).reshape(-1, 2048))
    for core in range(8):
        b, half = core // 2, core % 2
        xT = np.zeros((D, 2 * T), np.float32)
        if half == 0:
            xT[:, T:] = x[b, :T].T
        else:
            xT[:, :] = x[b].T
        v = vec.copy()
        o_, _ = c.cols["flag"]
        v[:, o_] = float(half)
        m = {"wall": wall, "masks": sh["masks"]}
        m["xT"] = xT
        m["vecs"] = v
        in_maps.append(m)
    if nc is None:
        nc = build(c)
    res = run_bass_kernel_spmd(nc, in_maps, core_ids=list(range(8)))
    out = np.zeros((B, S, D), np.float32)
    for core in range(8):
        b, half = core // 2, core % 2
        out[b, half * T:(half + 1) * T, :] = res.results[core]["yT"].T
    return out


def kernel(**inputs):
    return run(Cfg(), inputs)
```

```python
import contextlib
import numpy as np
import ml_dtypes
import concourse.bass as bass
import concourse.mybir as mybir
from concourse.bass_utils import run_bass_kernel_spmd

F32 = mybir.dt.float32
BF16 = mybir.dt.bfloat16
AF = mybir.ActivationFunctionType
ALU = mybir.AluOpType

ENGS = ["pe", "act", "dve", "pool", "sp"]
CH = 30000
CHD = 1900


class Res:
    __slots__ = ("name", "w", "r", "ndma", "dsems")

    def __init__(self, name):
        self.name = name
        self.w = None
        self.r = {}
        self.ndma = 0
        self.dsems = {}


class Prog:
    def __init__(self, nc):
        self.nc = nc
        self.es = contextlib.ExitStack()
        self.q = {e: [] for e in ENGS}
        self.cnt = {e: 0 for e in ENGS}
        self.seen = {e: {} for e in ENGS}
        self.engsem = {}
        self.nsem = 0

    def sem(self, name):
        self.nsem += 1
        return self.es.enter_context(self.nc.semaphore(name))

    def sbuf(self, name, shape, dtype):
        return self.es.enter_context(self.nc.sbuf_tensor(name, list(shape), dtype))

    def psum(self, name, shape, dtype=F32):
        return self.es.enter_context(self.nc.psum_tensor(name, list(shape), dtype))

    def _esem(self, e, c):
        if (e, c) not in self.engsem:
            self.engsem[(e, c)] = self.sem(f"s_{e}_{c}")
        return self.engsem[(e, c)]

    def _dsem(self, res, c):
        if c not in res.dsems:
            res.dsems[c] = self.sem(f"d_{res.name}_{c}")
        return res.dsems[c]

    def _need(self, eng, ev):
        if ev[0] == "e":
            _, e, k = ev
            if e == "pe" and eng == "pe":
                return []
            key = ("e", e)
            if self.seen[eng].get(key, 0) >= k:
                return []
            self.seen[eng][key] = k
            return [(self._esem(e, (k - 1) // CH), (k - 1) % CH + 1)]
        _, res, n = ev
        key = ("d", id(res))
        if self.seen[eng].get(key, 0) >= n:
            return []
        self.seen[eng][key] = n
        return [(self._dsem(res, (n - 1) // CHD), ((n - 1) % CHD + 1) * 16)]

    def op(self, eng, fn, reads=(), writes=(), dma=False, event=True):
        waits = []
        for r in reads:
            if r.w is not None:
                waits += self._need(eng, r.w)
        for w in writes:
            if w.w is not None:
                waits += self._need(eng, w.w)
            for ev in w.r.values():
                waits += self._need(eng, ev)
        inc = None
        if dma:
            tgt = writes[0]
            tgt.ndma += 1
            n = tgt.ndma
            c = (n - 1) // CHD
            if n > 1 and (n - 1) % CHD == 0:
                waits.append((self._dsem(tgt, c - 1), CHD * 16))
            ev = ("d", tgt, n)
            inc = (self._dsem(tgt, c), 16)
        elif event:
            self.cnt[eng] += 1
            k = self.cnt[eng]
            ev = ("e", eng, k)
            inc = (self._esem(eng, (k - 1) // CH), 1)
        else:
            ev = ("e", eng, self.cnt[eng] + 1)
        for r in reads:
            key = (ev[0], ev[1] if ev[0] == "e" else id(ev[1]))
            r.r[key] = ev
        for w in writes:
            w.w = ev
            w.r = {}
        self.q[eng].append((waits, fn, inc))

    def wait_all(self, eng, ress):
        waits = []
        for r in ress:
            if r.w is not None:
                waits += self._need(eng, r.w)
        self.q[eng].append((waits, None, None))

    def check(self):
        val = {}
        pos = {e: 0 for e in ENGS}
        progress = True
        while progress:
            progress = False
            for e in ENGS:
                q = self.q[e]
                while pos[e] < len(q):
                    waits, fn, inc = q[pos[e]]
                    if any(val.get(id(s_), 0) < v for (s_, v) in waits):
                        break
                    if inc is not None:
                        val[id(inc[0])] = val.get(id(inc[0]), 0) + inc[1]
                    pos[e] += 1
                    progress = True
        stuck = {e: pos[e] for e in ENGS if pos[e] < len(self.q[e])}
        if stuck:
            msg = []
            for e, i in stuck.items():
                waits = self.q[e][i][0]
                msg.append(f"{e}@{i}/{len(self.q[e])}: " + ", ".join(f"{s_.name}>={v} (now {val.get(id(s_), 0)})" for s_, v in waits
                                                                   if val.get(id(s_), 0) < v))
            raise RuntimeError("DEADLOCK in recorded program:\n" + "\n".join(msg))
        self.maxsem = max(val.values()) if val else 0

    def emit(self):
        self.check()
        nc = self.nc
        handles = {"pe": nc.tensor, "act": nc.scalar, "dve": nc.vector, "pool": nc.gpsimd, "sp": nc.sync}
        with nc.Block() as block:
            def run(e):
                h = handles[e]
                for waits, fn, inc in self.q[e]:
                    for (s, v) in waits:
                        h.wait_ge(s, v)
                    if fn is not None:
                        ins = fn(h)
                        if inc is not None:
                            ins.then_inc(inc[0], inc[1])

            @block.tensor
            def _(t):
                run("pe")

            @block.scalar
            def _(t):
                run("act")

            @block.vector
            def _(t):
                run("dve")

            @block.gpsimd
            def _(t):
                run("pool")

            @block.sync
            def _(t):
                run("sp")

    def barrier(self):
        for eng in ENGS:
            waits = []
            for e in ["pe", "act", "dve", "pool"]:
                if self.cnt[e] > 0:
                    waits += self._need(eng, ("e", e, self.cnt[e]))
            for r in self.allres:
                if r.ndma > 0:
                    waits += self._need(eng, ("d", r, r.ndma))
            self.q[eng].append((waits, None, None))
        for r in self.allres:
            r.w = None
            r.r = {}

    allres = []

    def res(self, name):
        r = Res(name)
        self.allres.append(r)
        return r

    def ress(self, name, n):
        return [self.res(f"{name}{i}") for i in range(n)]


EPS = 1e-6
LAMBDA_INIT0 = 0.8 - 0.6 * 1.0


class Cfg:
    def __init__(self, D=2048, T=2048, H=8, HQ=16, HKV=2, DFF=8192, NW=6):
        self.D, self.T, self.H, self.HQ, self.HKV, self.DFF, self.NW = D, T, H, HQ, HKV, DFF, NW
        self.TT = 512
        self.KC = D // 128
        self.NT = T // 512
        self.FC = DFF // 128
        self.NB = D // 256
        self.KG2 = min(16, self.FC)
        self.NG2 = self.FC // self.KG2
        self.GRP = HQ // HKV
        assert H * 128 + HQ * 64 == D and self.KC <= 16 and HQ % 2 == 0
        cols = {}
        off = 0
        KC = self.KC
        for nm, n in ([(f"g{l}_{k}", KC) for l in range(2) for k in ("premix", "postmix", "premlp", "postmlp")]
                      + [(f"convw{j}", KC) for j in range(4)] + [("convb", KC), ("gateb0", KC), ("gateb1", KC), ("lrul", KC),
                      ("subg", 1), ("sinks", HQ), ("lq1", 64), ("lk1", 64), ("lq2", 64), ("lk2", 64), ("flag", 1)]):
            cols[nm] = (off, n)
            off += n
        self.cols = cols
        self.NV = off


def build(cfg):
    c = cfg
    D, T, H, HQ, HKV, KC, NT, TT, FC, NB, NW = c.D, c.T, c.H, c.HQ, c.HKV, c.KC, c.NT, c.TT, c.FC, c.NB, c.NW
    T2 = 2 * T
    NCK = T2 // 128
    nc = bass.Bass("TRN2", target_bir_lowering=False)

    def din(name, shape, dtype=F32):
        return nc.dram_tensor(name, list(shape), dtype, kind="ExternalInput")

    def dsc(name, shape, dtype=F32):
        return nc.dram_tensor(name, list(shape), dtype)

    xT = din("xT", [D, T2])
    vecs = din("vecs", [128, c.NV])
    masks = din("masks", [128, 5 * 512])
    nA = 2 * H + 2 * HKV + H + HQ // 2
    warrs = {}
    WW = 2048
    woff = [0]

    def warr(name, nblk, kcb):
        warrs[name] = (woff[0] // WW, kcb)
        woff[0] += nblk * 128 * kcb * 128
        return (name, kcb)

    wA = warr("wA", nA, KC)
    wo = [None, None]
    w1 = [None, None]
    w2 = [None, None]
    wo[0] = warr("wo0", KC, KC)
    w1[0] = warr("w1_0", FC, KC)
    w2[0] = warr("w2_0", KC * c.NG2, c.KG2)
    wB = warr("wB", 2 * KC, KC)
    wg = warr("wg", 2 * NB * 2, 2)
    wo[1] = warr("wo1", KC, KC)
    w1[1] = warr("w1_1", FC, KC)
    w2[1] = warr("w2_1", KC * c.NG2, c.KG2)
    warr_order = ["wA", "wo0", "w1_0", "w2_0", "wB", "wg", "wo1", "w1_1", "w2_1"]
    R8 = woff[0] // WW
    c.Ltot = woff[0]
    wall_in = din("wall", [R8, WW])
    wfull = dsc("wfull", [R8, WW], BF16)
    yT = nc.dram_tensor("yT", [D, T], F32, kind="ExternalOutput")
    qT_d = dsc("qT_d", [H * 128, T], BF16)
    kT_d = dsc("kT_d", [H * 128, T2], BF16)
    v_d = dsc("v_d", [T2, H * 128], BF16)
    qsT_d = dsc("qsT_d", [HQ * 64, T], BF16)
    ksT_d = dsc("ksT_d", [HKV * 128, T2], BF16)
    vs_d = dsc("vs_d", [T2, HKV * 128], BF16)
    at_d = dsc("at_d", [D, T], BF16)
    xr_d = dsc("xr_d", [D, T], F32)
    y_d = dsc("y_d", [D, T], BF16)
    xb_d = dsc("xb_d", [D, T], F32)
    a_d = dsc("a_d", [D, T], F32)
    u_d = dsc("u_d", [D, T], F32)
    cc1_in = dsc("cc1_in", [128, KC * 4], F32)
    cc1_out = dsc("cc1_out", [256, KC * 4], F32)
    cc2_in = dsc("cc2_in", [128, 16], F32)
    cc2_out = dsc("cc2_out", [256, 16], F32)

    P = Prog(nc)
    P.allres = []
    R = P.res
    r_qT, r_kT, r_v, r_qsT, r_ksT, r_vs = R("qT"), R("kT"), R("v"), R("qsT"), R("ksT"), R("vs")
    r_at = P.ress("at", NT)
    r_xr = P.ress("xr", NT)
    r_y = P.ress("y", NT)
    r_xb = P.ress("xb", NT)
    r_a = P.ress("a", NT)
    r_u = P.ress("u", NT)
    r_cc1i, r_cc1o, r_cc2i, r_cc2o = R("cc1i"), R("cc1o"), R("cc2i"), R("cc2o")
    r_out = R("out")

    vec = P.sbuf("vec", [128, c.NV], F32); r_vec = R("vec")
    ones = P.sbuf("ones", [128, 128], BF16); r_ones = R("ones")
    onesf = P.sbuf("onesf", [128, 128], BF16); r_onesf = R("onesf")
    mk = P.sbuf("mk", [128, 5 * 512], BF16); r_mk = R("mk")
    Wt = [P.sbuf(f"W{i}", [128, 16, 128], BF16) for i in range(NW)]; r_W = P.ress("W", NW)
    rstd = P.sbuf("rstd", [128, TT], F32); r_rstd = R("rstd")
    NSQ, NTF, NSG = 3, 6, 4
    sq = [P.sbuf(f"sq{i}", [128, TT], BF16) for i in range(NSQ)]; r_sq = P.ress("sq", NSQ)
    tf = [P.sbuf(f"tf{i}", [128, TT + 4], F32) for i in range(NTF)]; r_tf = P.ress("tf", NTF)
    sg = [P.sbuf(f"sg{i}", [128, TT], BF16) for i in range(NSG)]; r_sg = P.ress("sg", NSG)
    sm = P.sbuf("sm", [128, 64], F32); r_sm = R("sm")
    cst = P.sbuf("cst", [128, 8 + HQ + KC], F32); r_cst = R("cst")
    hal = P.sbuf("hal", [128, KC, 4], F32); r_hal = R("hal")
    hin = P.sbuf("hin", [128, 16], F32); r_hin = R("hin")
    hfin = P.sbuf("hfin", [128, KC], F32); r_hfin = P.ress("hfin", KC)
    bank = [P.psum(f"bk{i}", [128, 512], F32) for i in range(8)]; r_bk = P.ress("bk", 8)

    cnt = {"w": 0, "sq": 0, "tf": 0, "sg": 0, "bk": 0, "ev": 0}

    def col(name, i=0, n=1):
        o, _ = c.cols[name]
        return vec[:, o + i:o + i + n]

    def nxt(kind, n):
        i = cnt[kind] % n
        cnt[kind] += 1
        return i

    def dma(q, out, in_, reads, writes):
        P.op(q, lambda e: e.dma_start(out=out, in_=in_), reads=reads, writes=writes, dma=True)

    def mm(out, lhsT, rhs, start, stop, reads, writes, event=None):
        P.op("pe", lambda e: e.matmul(out, lhsT=lhsT, rhs=rhs, start=start, stop=stop), reads=reads, writes=writes,
             event=stop if event is None else event)

    def act(out, in_, func, reads, writes, bias=None, scale=None, eng="act"):
        kw = {}
        if bias is not None:
            kw["bias"] = bias
        if scale is not None:
            kw["scale"] = scale
        P.op(eng, lambda e: e.activation(out=out, in_=in_, func=func, **kw), reads=reads, writes=writes)

    def tt(eng, out, in0, in1, op, reads, writes):
        P.op(eng, lambda e: e.tensor_tensor(out=out, in0=in0, in1=in1, op=op), reads=reads, writes=writes)

    def ts(eng, out, in0, s1, s2, op0, op1, reads, writes):
        if op1 is None:
            P.op(eng, lambda e: e.tensor_scalar(out=out, in0=in0, scalar1=s1, scalar2=None, op0=op0), reads=reads, writes=writes)
        else:
            P.op(eng, lambda e: e.tensor_scalar(out=out, in0=in0, scalar1=s1, scalar2=s2, op0=op0, op1=op1), reads=reads, writes=writes)

    def stt(eng, out, in0, scalar, in1, op0, op1, reads, writes):
        P.op(eng, lambda e: e.scalar_tensor_tensor(out=out, in0=in0, scalar=scalar, in1=in1, op0=op0, op1=op1), reads=reads, writes=writes)

    def cp(eng, out, in_, reads, writes):
        P.op(eng, lambda e: e.tensor_copy(out=out, in_=in_), reads=reads, writes=writes)

    r_wf = {}

    def wres(name, rel_row):
        key = (name, rel_row // 512 if name == "wA" else 0)
        if key not in r_wf:
            r_wf[key] = R(f"{name}_f{key[1]}")
        return r_wf[key]

    def wload(src, kcb):
        (name, kcb_), blk = src
        assert kcb_ == kcb
        roff = warrs[name][0] + blk * 8 * kcb
        src_ap = wfull.ap()[roff:roff + 8 * kcb, :].rearrange("a (b k c) -> (a b) k c", k=kcb, c=128)
        i = nxt("w", NW)
        dma("pool", Wt[i][:, 0:kcb, :], src_ap, [wres(name, blk * 8 * kcb)], [r_W[i]])
        return Wt[i], r_W[i]

    evac_toggle = [0]

    def evac_copy(out, in_, reads, writes):
        evac_toggle[0] ^= 1
        if evac_toggle[0]:
            act(out, in_, AF.Copy, reads, writes)
        else:
            cp("dve", out, in_, reads, writes)

    cast_chunks = []
    row_end = {}
    names_sorted = sorted(warrs, key=lambda n_: warrs[n_][0])
    for ai, name in enumerate(names_sorted):
        r0 = warrs[name][0]
        r1 = warrs[names_sorted[ai + 1]][0] if ai + 1 < len(names_sorted) else R8
        for a in range(r0, r1, 128):
            cast_chunks.append((name, a, min(a + 128, r1)))
    cast_pos = [0]

    def cast_some(n=None, upto=None):
        while cast_pos[0] < len(cast_chunks):
            name, a, b_ = cast_chunks[cast_pos[0]]
            if upto is not None and warrs[name][0] > warrs[upto][0]:
                break
            if upto is None and n is not None and n <= 0:
                break
            i = nxt("w", NW)
            nr = b_ - a
            dma("pool", Wt[i][0:nr, :, :], wall_in.ap()[a:b_, :].rearrange("p (k c) -> p k c", c=128), [], [r_W[i]])
            dma("sp", wfull.ap()[a:b_, :].rearrange("p (k c) -> p k c", c=128), Wt[i][0:nr, :, :], [r_W[i]], [wres(name, a - warrs[name][0])])
            cast_pos[0] += 1
            if n is not None:
                n -= 1

    cast_some(upto="wA")

    dma("sp", vec[:, :], vecs[:, :], [], [r_vec])
    dma("pool", mk[:, :], masks[:, :], [], [r_mk])
    P.op("dve", lambda e: e.memset(ones[:], 1.0), writes=[r_ones])
    ts("dve", onesf[:], ones[:], col("flag"), None, ALU.mult, None, [r_ones, r_vec], [r_onesf])
    o1, _ = c.cols["lq1"]; o2, _ = c.cols["lk1"]; o3, _ = c.cols["lq2"]; o4, _ = c.cols["lk2"]
    tt("dve", sm[:, 0:64], vec[:, o1:o1 + 64], vec[:, o2:o2 + 64], ALU.mult, [r_vec], [r_sm])
    P.op("dve", lambda e: e.reduce_sum(out=cst[:, 4:5], in_=sm[:, 0:64], axis=mybir.AxisListType.X), reads=[r_sm], writes=[r_cst])
    tt("dve", sm[:, 0:64], vec[:, o3:o3 + 64], vec[:, o4:o4 + 64], ALU.mult, [r_vec, r_cst], [r_sm])
    P.op("dve", lambda e: e.reduce_sum(out=cst[:, 5:6], in_=sm[:, 0:64], axis=mybir.AxisListType.X), reads=[r_sm], writes=[r_cst])
    act(cst[:, 4:6], cst[:, 4:6], AF.Exp, [r_cst], [r_cst])
    tt("dve", cst[:, 6:7], cst[:, 5:6], cst[:, 4:5], ALU.subtract, [r_cst], [r_cst])
    ts("dve", cst[:, 0:1], cst[:, 6:7], -LAMBDA_INIT0, None, ALU.add, None, [r_cst], [r_cst])
    ts("dve", cst[:, 1:2], col("subg"), 1.0 - LAMBDA_INIT0, None, ALU.mult, None, [r_vec, r_cst], [r_cst])
    ES = 8
    so, _ = c.cols["sinks"]
    act(cst[:, ES:ES + HQ], vec[:, so:so + HQ], AF.Exp, [r_vec, r_cst], [r_cst])
    C1 = ES + HQ
    lo, _ = c.cols["lrul"]
    act(sm[:, 0:KC], vec[:, lo:lo + KC], AF.Exp, [r_vec], [r_sm], scale=-1.0)
    act(sm[:, 16:16 + KC], sm[:, 0:KC], AF.Ln, [r_sm], [r_sm], bias=1.0)
    ts("dve", sm[:, 32:32 + KC], sm[:, 0:KC], 1.0, None, ALU.min, None, [r_sm], [r_sm])
    ts("dve", sm[:, 48:48 + KC], sm[:, 32:32 + KC], -0.25, 1.0 / 3.0, ALU.mult, ALU.add, [r_sm], [r_sm])
    tt("dve", sm[:, 48:48 + KC], sm[:, 48:48 + KC], sm[:, 32:32 + KC], ALU.mult, [r_sm], [r_sm])
    ts("dve", sm[:, 48:48 + KC], sm[:, 48:48 + KC], -0.5, None, ALU.add, None, [r_sm], [r_sm])
    tt("dve", sm[:, 48:48 + KC], sm[:, 48:48 + KC], sm[:, 32:32 + KC], ALU.mult, [r_sm], [r_sm])
    ts("dve", sm[:, 48:48 + KC], sm[:, 48:48 + KC], 1.0, None, ALU.add, None, [r_sm], [r_sm])
    tt("dve", sm[:, 48:48 + KC], sm[:, 48:48 + KC], sm[:, 32:32 + KC], ALU.mult, [r_sm], [r_sm])
    ts("dve", sm[:, 32:32 + KC], sm[:, 0:KC], 0.05, None, ALU.is_lt, None, [r_sm], [r_sm])
    tt("dve", sm[:, 48:48 + KC], sm[:, 48:48 + KC], sm[:, 16:16 + KC], ALU.subtract, [r_sm], [r_sm])
    tt("dve", sm[:, 48:48 + KC], sm[:, 48:48 + KC], sm[:, 32:32 + KC], ALU.mult, [r_sm], [r_sm])
    tt("dve", sm[:, 48:48 + KC], sm[:, 48:48 + KC], sm[:, 16:16 + KC], ALU.add, [r_sm], [r_sm])
    ts("dve", cst[:, C1:C1 + KC], sm[:, 48:48 + KC], -8.0, None, ALU.mult, None, [r_sm, r_cst], [r_cst])

    def rms_stats(src_chunk, src_reads, n_chunks, dim):
        b = 7
        for kc in range(n_chunks):
            i = nxt("sq", NSQ)
            act(sq[i][:, :], src_chunk(kc), AF.Square, src_reads(kc), [r_sq[i]])
            mm(bank[b][:, :], ones[:, :], sq[i][:, :], kc == 0, kc == n_chunks - 1, [r_ones, r_sq[i]], [r_bk[b]], event=True)
        ts("dve", rstd[:, :], bank[b][:, :], 1.0 / dim, EPS, ALU.mult, ALU.add, [r_bk[b]], [r_rstd])
        act(rstd[:, :], rstd[:, :], AF.Sqrt, [r_rstd], [r_rstd])
        P.op("dve", lambda e: e.reciprocal(out=rstd[:, :], in_=rstd[:, :]), reads=[r_rstd], writes=[r_rstd])

    def norm_to_bf16(Xt, r_X, gname, HBt, r_HB):
        rms_stats(lambda kc: Xt[:, kc, :], lambda kc: [r_X[kc]], KC, D)
        for kc in range(KC):
            stt("dve", HBt[:, kc, :], Xt[:, kc, :], col(gname, kc), rstd[:, :], ALU.mult, ALU.mult,
                [r_X[kc], r_vec, r_rstd], [r_HB[kc]])

    def proj_fm(HBt, r_HB, wsrc, kcb, k0, start, stop, b):
        Wb, rW = wload(wsrc, kcb)
        for kc in range(kcb):
            mm(bank[b][:, :], Wb[:, kc, :], HBt[:, k0 + kc, :], start and kc == 0, stop and kc == kcb - 1,
               [rW, r_HB[k0 + kc]], [r_bk[b]], event=(kc == kcb - 1))

    def add_norm_residual(Xt, r_X, Ft, r_F, gname):
        rms_stats(lambda kc: Ft[:, kc, :], lambda kc: [r_F[kc]], KC, D)
        for kc in range(KC):
            stt("dve", Ft[:, kc, :], Ft[:, kc, :], col(gname, kc), rstd[:, :], ALU.mult, ALU.mult,
                [r_F[kc], r_vec, r_rstd], [r_F[kc]])
            tt("dve", Xt[:, kc, :], Xt[:, kc, :], Ft[:, kc, :], ALU.add, [r_X[kc], r_F[kc]], [r_X[kc]])

    def mixout_mlp(l, Xt, r_X, Ft, r_F, HBt, r_HB, Zt, r_Z):
        for m in range(KC):
            b = nxt("bk", 6)
            proj_fm(HBt, r_HB, (wo[l], m), KC, 0, True, True, b)
            evac_copy(Ft[:, m, :], bank[b][:, :], [r_bk[b]], [r_F[m]])
        add_norm_residual(Xt, r_X, Ft, r_F, f"g{l}_postmix")
        norm_to_bf16(Xt, r_X, f"g{l}_premlp", HBt, r_HB)
        for m in range(FC):
            b = nxt("bk", 6)
            proj_fm(HBt, r_HB, (w1[l], m), KC, 0, True, True, b)
            i = nxt("tf", NTF)
            act(tf[i][:, 0:TT], bank[b][:, :], AF.Relu, [r_bk[b]], [r_tf[i]])
            tt("dve", Zt[:, m, :], tf[i][:, 0:TT], tf[i][:, 0:TT], ALU.mult, [r_tf[i]], [r_Z[m]])
        for m in range(KC):
            b = nxt("bk", 6)
            for g in range(c.NG2):
                proj_fm(Zt, r_Z, (w2[l], m * c.NG2 + g), c.KG2, g * c.KG2, g == 0, g == c.NG2 - 1, b)
            evac_copy(Ft[:, m, :], bank[b][:, :], [r_bk[b]], [r_F[m]])
        add_norm_residual(Xt, r_X, Ft, r_F, f"g{l}_postmlp")

    xT3 = xT.ap().rearrange("(kc p) n -> p kc n", p=128)

    with contextlib.ExitStack() as ph:
        X = ph.enter_context(nc.sbuf_tensor("X_a", [128, KC, TT], F32)); r_X = P.ress("Xa", KC)
        HB = ph.enter_context(nc.sbuf_tensor("HB_a", [128, KC, TT], BF16)); r_HB = P.ress("HBa", KC)
        v3 = v_d.ap().rearrange("(c p) n -> p c n", p=128)
        vs3 = vs_d.ap().rearrange("(c p) n -> p c n", p=128)
        for t in range(2 * NT):
            own = t >= NT
            tsl = slice(t * TT, (t + 1) * TT)
            dma("sp", X[:, :, :], xT3[:, :, tsl], [], r_X)
            norm_to_bf16(X, r_X, "g0_premix", HB, r_HB)

            def fm_block(widx, dst, r_dst, row0, csl):
                b = nxt("bk", 6)
                proj_fm(HB, r_HB, (wA, widx), KC, 0, True, True, b)
                i = nxt("sg", NSG)
                evac_copy(sg[i][:, :], bank[b][:, :], [r_bk[b]], [r_sg[i]])
                dma("sp", dst[row0:row0 + 128, csl], sg[i][:, :], [r_sg[i]], [r_dst])

            def tm_block(widx, dst3, r_dst, col0):
                b = nxt("bk", 6)
                Wb, rW = wload((wA, widx), KC)
                for cc in range(4):
                    for kc in range(KC):
                        mm(bank[b][:, cc * 128:(cc + 1) * 128], HB[:, kc, cc * 128:(cc + 1) * 128], Wb[:, kc, :],
                           kc == 0, kc == KC - 1, [rW, r_HB[kc]], [r_bk[b]])
                i = nxt("sg", NSG)
                evac_copy(sg[i][:, :], bank[b][:, :], [r_bk[b]], [r_sg[i]])
                dma("sp", dst3[:, t * 4:(t + 1) * 4, col0:col0 + 128], sg[i][:, :].rearrange("p (c n) -> p c n", n=128),
                    [r_sg[i]], [r_dst])

            for h in range(H):
                fm_block(h, kT_d, r_kT, h * 128, tsl)
            for h in range(H):
                tm_block(H + h, v3, r_v, h * 128)
            for g in range(HKV):
                fm_block(2 * H + g, ksT_d, r_ksT, g * 128, tsl)
            for g in range(HKV):
                tm_block(2 * H + HKV + g, vs3, r_vs, g * 128)
            if own:
                osl = slice((t - NT) * TT, (t - NT + 1) * TT)
                for h in range(H):
                    fm_block(2 * H + 2 * HKV + h, qT_d, r_qT, h * 128, osl)
                for j in range(HQ // 2):
                    fm_block(2 * H + 2 * HKV + H + j, qsT_d, r_qsT, j * 128, osl)
            n_l0 = sum(1 for (nm_, _a, _b) in cast_chunks if nm_ in ("wo0", "w1_0", "w2_0"))
            cast_some(n=-(-n_l0 // (2 * NT)))
        cast_some(upto="w2_0")
        P.barrier()

    with contextlib.ExitStack() as ph:
        kTb = [ph.enter_context(nc.sbuf_tensor(f"kTb{i}", [128, T2], BF16)) for i in range(2)]; r_kTb = P.ress("kTb", 2)
        vhb = [ph.enter_context(nc.sbuf_tensor(f"vhb{i}", [128, NCK, 128], BF16)) for i in range(2)]; r_vhb = P.ress("vhb", 2)
        qTb = [ph.enter_context(nc.sbuf_tensor(f"qTb{i}", [128, T], BF16)) for i in range(2)]; r_qTb = P.ress("qTb", 2)
        NPT = 4
        pT = [ph.enter_context(nc.sbuf_tensor(f"pT{i}", [128, TT], BF16)) for i in range(NPT)]; r_pT = P.ress("pT", NPT)
        cpt = [0]
        SC = 0.125
        def head_loads(h_):
            s_ = h_ % 2
            dma("sp", kTb[s_][:, :], kT_d[h_ * 128:(h_ + 1) * 128, :], [r_kT], [r_kTb[s_]])
            dma("sp", vhb[s_][:, :, :], v3[:, :, h_ * 128:(h_ + 1) * 128], [r_v], [r_vhb[s_]])
            dma("sp", qTb[s_][:, :], qT_d[h_ * 128:(h_ + 1) * 128, :], [r_qT], [r_qTb[s_]])

        head_loads(0)
        for h in range(H):
            if h + 1 < H:
                head_loads(h + 1)
            cast_some(n=-(-(len(cast_chunks) - cast_pos[0]) // (H - h)))
            s = h % 2
            for tq in range(NT):
                nk = (T + (tq + 1) * TT) // 128
                kd0 = (T + tq * TT) // 128
                qsl = slice(tq * TT, (tq + 1) * TT)

                def qk(kc):
                    for j in range(2):
                        b = (kc % 2) * 2 + j
                        mm(bank[b][:, :], kTb[s][j * 64:(j + 1) * 64, kc * 128:(kc + 1) * 128], qTb[s][j * 64:(j + 1) * 64, qsl],
                           True, True, [r_kTb[s], r_qTb[s]], [r_bk[b]])

                def pv(kc):
                    for j in range(2):
                        b = (kc % 2) * 2 + j
                        ip = cpt[0] % NPT
                        cpt[0] += 1
                        act(pT[ip][:, :], bank[b][:, :], AF.Exp, [r_bk[b]], [r_pT[ip]], scale=SC)
                        if kc >= kd0:
                            o = kc - kd0
                            tt("dve", pT[ip][:, :], pT[ip][:, :], mk[:, o * 512:(o + 1) * 512], ALU.mult, [r_pT[ip], r_mk], [r_pT[ip]])
                        last = kc == nk - 1
                        mm(bank[4 + j][:, :], vhb[s][:, kc, :], pT[ip][:, :], kc == 0, last, [r_vhb[s], r_pT[ip]], [r_bk[4 + j]], event=False)
                        on = onesf if kc < T // 128 else ones
                        mm(bank[6 + j][:, :], on[:, :], pT[ip][:, :], kc == 0, last, [r_ones, r_onesf, r_pT[ip]], [r_bk[6 + j], r_bk[4 + j]],
                           event=True)

                qk(0)
                for kc in range(nk):
                    if kc + 1 < nk:
                        qk(kc + 1)
                    pv(kc)
                f = [nxt("tf", NTF) for _ in range(4)]
                for j in range(2):
                    P.op("dve", (lambda jj, ff: (lambda e: e.reciprocal(out=tf[ff][:, 0:TT], in_=bank[6 + jj][:, :])))(j, f[j]),
                         reads=[r_bk[6 + j]], writes=[r_tf[f[j]]])
                    tt("dve", tf[f[j]][:, 0:TT], bank[4 + j][:, :], tf[f[j]][:, 0:TT], ALU.mult, [r_bk[4 + j], r_tf[f[j]]], [r_tf[f[j]]])
                stt("dve", tf[f[2]][:, 0:TT], tf[f[1]][:, 0:TT], cst[:, 0:1], tf[f[0]][:, 0:TT], ALU.mult, ALU.add,
                    [r_tf[f[0]], r_tf[f[1]], r_cst], [r_tf[f[2]]])
                rms_stats(lambda kc: tf[f[2]][:, 0:TT], lambda kc: [r_tf[f[2]]], 1, 128)
                i = nxt("sg", NSG)
                stt("dve", sg[i][:, :], tf[f[2]][:, 0:TT], cst[:, 1:2], rstd[:, :], ALU.mult, ALU.mult,
                    [r_tf[f[2]], r_cst, r_rstd], [r_sg[i]])
                dma("sp", at_d[h * 128:(h + 1) * 128, qsl], sg[i][:, :], [r_sg[i]], [r_at[tq]])

        mks = mk[:, 4 * 512:5 * 512]
        for bq in range(HQ // 2):
            g = (2 * bq) // c.GRP
            s = bq % 2
            dma("sp", kTb[s][:, :], ksT_d[g * 128:(g + 1) * 128, :], [r_ksT], [r_kTb[s]])
            dma("sp", vhb[s][:, :, :], vs3[:, :, g * 128:(g + 1) * 128], [r_vs], [r_vhb[s]])
            dma("sp", qTb[s][:, :], qsT_d[bq * 128:(bq + 1) * 128, :], [r_qsT], [r_qTb[s]])
            for tq in range(NT):
                qsl = slice(tq * TT, (tq + 1) * TT)
                isg = nxt("sg", NSG)
                for hh in range(2):
                    ps = slice(hh * 64, (hh + 1) * 64)
                    hq = 2 * bq + hh
                    for half in range(2):
                        b = half
                        for qq in range(2):
                            i = tq * 4 + half * 2 + qq
                            ci = T // 128 + i
                            for w_, kc in enumerate((ci - 1, ci)):
                                cs = slice((qq * 2 + w_) * 128, (qq * 2 + w_ + 1) * 128)
                                mm(bank[b][:, cs], kTb[s][ps, kc * 128:(kc + 1) * 128], qTb[s][ps, i * 128:(i + 1) * 128],
                                   True, True, [r_kTb[s], r_qTb[s]], [r_bk[b]], event=(qq == 1 and w_ == 1))
                        ip = cpt[0] % NPT
                        cpt[0] += 1
                        act(pT[ip][:, :], bank[b][:, :], AF.Exp, [r_bk[b]], [r_pT[ip]], scale=SC)
                        tt("dve", pT[ip][:, :], pT[ip][:, :], mks, ALU.mult, [r_pT[ip], r_mk], [r_pT[ip]])
                        for qq in range(2):
                            i = tq * 4 + half * 2 + qq
                            ci = T // 128 + i
                            osl = slice((half * 2 + qq) * 128, (half * 2 + qq + 1) * 128)
                            for w_, kc in enumerate((ci - 1, ci)):
                                cs = slice((qq * 2 + w_) * 128, (qq * 2 + w_ + 1) * 128)
                                mm(bank[4][:, osl], vhb[s][:, kc, :], pT[ip][:, cs], w_ == 0, w_ == 1, [r_vhb[s], r_pT[ip]], [r_bk[4]], event=False)
                                on = onesf if kc < T // 128 else ones
                                mm(bank[6][:, osl], on[:, :], pT[ip][:, cs], w_ == 0, w_ == 1, [r_ones, r_onesf, r_pT[ip]], [r_bk[6], r_bk[4]],
                                   event=(w_ == 1))
                    f0 = nxt("tf", NTF)
                    ts("dve", tf[f0][ps, 0:TT], bank[6][ps, :], cst[ps, ES + hq:ES + hq + 1], None, ALU.add, None, [r_bk[6], r_cst], [r_tf[f0]])
                    P.op("dve", (lambda ff, pp: (lambda e: e.reciprocal(out=tf[ff][pp, 0:TT], in_=tf[ff][pp, 0:TT])))(f0, ps),
                         reads=[r_tf[f0]], writes=[r_tf[f0]])
                    tt("dve", sg[isg][ps, :], bank[4][ps, :], tf[f0][ps, 0:TT], ALU.mult, [r_bk[4], r_tf[f0]], [r_sg[isg]])
                dma("sp", at_d[H * 128 + bq * 128:H * 128 + (bq + 1) * 128, qsl], sg[isg][:, :], [r_sg[isg]], [r_at[tq]])
        P.barrier()

    at3 = at_d.ap().rearrange("(kc p) n -> p kc n", p=128)
    xr3 = xr_d.ap().rearrange("(kc p) n -> p kc n", p=128)
    xb3 = xb_d.ap().rearrange("(kc p) n -> p kc n", p=128)
    yT3 = yT.ap().rearrange("(kc p) n -> p kc n", p=128)
    PAIRS = [[2 * i, 2 * i + 1] for i in range(4)]
    with contextlib.ExitStack() as ph:
        X = ph.enter_context(nc.sbuf_tensor("X_c", [128, KC, TT], F32)); r_X = P.ress("Xc", KC)
        Fb = ph.enter_context(nc.sbuf_tensor("F_c", [128, KC, TT], F32)); r_F = P.ress("Fc", KC)
        HB = ph.enter_context(nc.sbuf_tensor("HB_c", [128, KC, TT], BF16)); r_HB = P.ress("HBc", KC)
        Z = ph.enter_context(nc.sbuf_tensor("Z_c", [128, FC, TT], BF16)); r_Z = P.ress("Zc", FC)
        for t in range(NT):
            tsl = slice(t * TT, (t + 1) * TT)
            dma("sp", X[:, :, :], xT3[:, :, T + t * TT:T + (t + 1) * TT], [], r_X)
            dma("sp", HB[:, :, :], at3[:, :, tsl], [r_at[t]], r_HB)
            mixout_mlp(0, X, r_X, Fb, r_F, HB, r_HB, Z, r_Z)
            dma("sp", xr3[:, :, tsl], X[:, :, :], r_X, [r_xr[t]])
            norm_to_bf16(X, r_X, "g1_premix", HB, r_HB)
            for m in range(KC):
                b = nxt("bk", 6)
                proj_fm(HB, r_HB, (wB, m), KC, 0, True, True, b)
                i = nxt("sg", NSG)
                act(sg[i][:, :], bank[b][:, :], AF.Gelu_apprx_tanh, [r_bk[b]], [r_sg[i]])
                dma("sp", y_d[m * 128:(m + 1) * 128, tsl], sg[i][:, :], [r_sg[i]], [r_y[t]])
            for m in range(KC):
                b = nxt("bk", 6)
                proj_fm(HB, r_HB, (wB, KC + m), KC, 0, True, True, b)
                i = nxt("tf", NTF)
                evac_copy(tf[i][:, 0:TT], bank[b][:, :], [r_bk[b]], [r_tf[i]])
                dma("sp", xb_d[m * 128:(m + 1) * 128, tsl], tf[i][:, 0:TT], [r_tf[i]], [r_xb[t]])
                if t == NT - 1:
                    dma("sp", cc1_in[:, m * 4:m * 4 + 3], tf[i][:, TT - 3:TT], [r_tf[i]], [r_cc1i])
        P.barrier()

    P.op("pool", lambda e: e.collective_compute("AllGather", ALU.bypass, replica_groups=PAIRS,
                                                ins=[cc1_in.ap().opt()], outs=[cc1_out.ap().opt()]),
         reads=[r_cc1i], writes=[r_cc1o])
    dma("sp", hal[:, :, :], cc1_out.ap()[0:128, :].rearrange("p (k c) -> p k c", c=4), [r_cc1o], [r_hal])
    ts("dve", hal[:, :, :], hal[:, :, :], col("flag"), None, ALU.mult, None, [r_hal, r_vec], [r_hal])

    with contextlib.ExitStack() as ph:
        XB = ph.enter_context(nc.sbuf_tensor("XB_e", [128, KC, TT + 4], F32)); r_XB = P.ress("XBe", KC)
        XC = ph.enter_context(nc.sbuf_tensor("XC_e", [128, KC, TT], F32)); r_XC = P.ress("XCe", KC)
        XCB = ph.enter_context(nc.sbuf_tensor("XCB_e", [128, KC, TT], BF16)); r_XCB = P.ress("XCBe", KC)
        Rg = [ph.enter_context(nc.sbuf_tensor(f"Rg{i}_e", [128, KC, TT], F32)) for i in range(2)]
        r_Rg = [P.ress(f"Rg{i}e", KC) for i in range(2)]
        for t in range(NT):
            tsl = slice(t * TT, (t + 1) * TT)
            if t == 0:
                cp("pool", XB[:, :, 0:3], hal[:, :, 0:3], [r_hal], r_XB)
            else:
                cp("pool", XB[:, :, 0:3], XB[:, :, TT:TT + 3], r_XB, r_XB)
            dma("sp", XB[:, :, 3:3 + TT], xb3[:, :, tsl], [r_xb[t]], r_XB)
            for cc in range(KC):
                act(XC[:, cc, :], XB[:, cc, 3:3 + TT], AF.Identity, [r_XB[cc], r_vec], [r_XC[cc]],
                    bias=col("convb", cc), scale=col("convw3", cc))
                for j in range(3):
                    stt("dve", XC[:, cc, :], XB[:, cc, j:j + TT], col(f"convw{j}", cc), XC[:, cc, :], ALU.mult, ALU.add,
                        [r_XB[cc], r_XC[cc], r_vec], [r_XC[cc]])
                cp("pool", XCB[:, cc, :], XC[:, cc, :], [r_XC[cc]], [r_XCB[cc]])
            for n in range(NB):
                for m in range(2):
                    co = 2 * n + m
                    for gi in range(2):
                        b = nxt("bk", 6)
                        Wb, rW = wload((wg, (gi * NB + n) * 2 + m), 2)
                        for kc in range(2):
                            mm(bank[b][:, :], Wb[:, kc, :], XCB[:, 2 * n + kc, :], kc == 0, kc == 1, [rW, r_XCB[2 * n + kc]], [r_bk[b]])
                        act(Rg[gi][:, co, :], bank[b][:, :], AF.Sigmoid, [r_bk[b], r_vec], [r_Rg[gi][co]], bias=col(f"gateb{gi}", co))
            for co in range(KC):
                act(Rg[0][:, co, :], Rg[0][:, co, :], AF.Exp, [r_Rg[0][co], r_cst], [r_Rg[0][co]], scale=cst[:, C1 + co:C1 + co + 1])
            for co in range(KC):
                it, iu, is_ = [nxt("tf", NTF) for _ in range(3)]
                A_, T_, U_, S_ = Rg[0][:, co, :], tf[it][:, 0:TT], tf[iu][:, 0:TT], tf[is_][:, 0:TT]
                dma("sp", a_d[co * 128:(co + 1) * 128, tsl], A_, [r_Rg[0][co]], [r_a[t]])
                tt("dve", T_, A_, A_, ALU.mult, [r_Rg[0][co]], [r_tf[it]])
                act(T_, T_, AF.Sqrt, [r_tf[it]], [r_tf[it]], bias=1.0, scale=-1.0)
                tt("pool", Rg[1][:, co, :], Rg[1][:, co, :], XC[:, co, :], ALU.mult, [r_Rg[1][co], r_XC[co]], [r_Rg[1][co]])
                tt("dve", U_, T_, Rg[1][:, co, :], ALU.mult, [r_tf[it], r_Rg[1][co]], [r_tf[iu]])
                dma("sp", u_d[co * 128:(co + 1) * 128, tsl], U_, [r_tf[iu]], [r_u[t]])
                init = 0.0 if t == 0 else hfin[:, co:co + 1]
                P.op("dve", (lambda o_, a_, u_, i_: (lambda e: e.tensor_tensor_scan(out=o_, data0=a_, data1=u_, initial=i_,
                                                                                    op0=ALU.mult, op1=ALU.add)))(S_, A_, U_, init),
                     reads=[r_Rg[0][co], r_tf[iu], r_hfin[co]], writes=[r_tf[is_]])
                cp("dve", hfin[:, co:co + 1], tf[is_][:, TT - 1:TT], [r_tf[is_]], [r_hfin[co]])
        dma("sp", cc2_in[:, 0:KC], hfin[:, :], r_hfin, [r_cc2i])
        P.barrier()

    P.op("pool", lambda e: e.collective_compute("AllGather", ALU.bypass, replica_groups=PAIRS,
                                                ins=[cc2_in.ap().opt()], outs=[cc2_out.ap().opt()]),
         reads=[r_cc2i], writes=[r_cc2o])
    dma("sp", hin[:, 0:KC], cc2_out[0:128, 0:KC], [r_cc2o], [r_hin])
    ts("dve", hin[:, 0:KC], hin[:, 0:KC], col("flag"), None, ALU.mult, None, [r_hin, r_vec], [r_hin])

    with contextlib.ExitStack() as ph:
        X = ph.enter_context(nc.sbuf_tensor("X_f", [128, KC, TT], F32)); r_X = P.ress("Xf", KC)
        Fb = ph.enter_context(nc.sbuf_tensor("F_f", [128, KC, TT], F32)); r_F = P.ress("Ff", KC)
        HB = ph.enter_context(nc.sbuf_tensor("HB_f", [128, KC, TT], BF16)); r_HB = P.ress("HBf", KC)
        Z = ph.enter_context(nc.sbuf_tensor("Z_f", [128, FC, TT], BF16)); r_Z = P.ress("Zf", FC)
        for t in range(NT):
            tsl = slice(t * TT, (t + 1) * TT)
            dma("sp", X[:, :, :], xr3[:, :, tsl], [r_xr[t]], r_X)
            for co in range(KC):
                ia, iu, is_ = [nxt("tf", NTF) for _ in range(3)]
                iy = nxt("sg", NSG)
                A_, U_, S_ = tf[ia][:, 0:TT], tf[iu][:, 0:TT], tf[is_][:, 0:TT]
                dma("sp", A_, a_d[co * 128:(co + 1) * 128, tsl], [r_a[t]], [r_tf[ia]])
                dma("sp", U_, u_d[co * 128:(co + 1) * 128, tsl], [r_u[t]], [r_tf[iu]])
                dma("sp", sg[iy][:, :], y_d[co * 128:(co + 1) * 128, tsl], [r_y[t]], [r_sg[iy]])
                init = hin[:, co:co + 1] if t == 0 else hfin[:, co:co + 1]
                P.op("dve", (lambda o_, a_, u_, i_: (lambda e: e.tensor_tensor_scan(out=o_, data0=a_, data1=u_, initial=i_,
                                                                                    op0=ALU.mult, op1=ALU.add)))(S_, A_, U_, init),
                     reads=[r_tf[ia], r_tf[iu], r_hfin[co], r_hin], writes=[r_tf[is_]])
                cp("dve", hfin[:, co:co + 1], tf[is_][:, TT - 1:TT], [r_tf[is_]], [r_hfin[co]])
                tt("dve", HB[:, co, :], S_, sg[iy][:, :], ALU.mult, [r_tf[is_], r_sg[iy]], [r_HB[co]])
            mixout_mlp(1, X, r_X, Fb, r_F, HB, r_HB, Z, r_Z)
            dma("sp", yT3[:, :, tsl], X[:, :, :], r_X, [r_out])
    P.wait_all("sp", [r_out])
    P.emit()
    P.es.close()
    return nc


def blockify(W, kcb):
    Kin, N = W.shape
    G, J = Kin // (128 * kcb), N // 128
    return np.ascontiguousarray(W.reshape(G, kcb, 128, J, 128).transpose(3, 0, 2, 1, 4).reshape(J * G, 128, kcb, 128))


def fm(v, KC):
    return np.asarray(v, np.float32).reshape(KC, 128).T


def prep_shared(cfg, inp):
    c = cfg
    D, H, HQ, HKV, KC = c.D, c.H, c.HQ, c.HKV, c.KC
    f = lambda a: np.asarray(a, np.float32)
    w_in = f(inp["even_w_in"])[0]
    QK, DV, SQ, SKV = H * 128, H * 128, HQ * 64, HKV * 64
    o = np.cumsum([0, QK, QK, DV, SQ, SKV, SKV])
    qa, ka, va, qs, ks, vs = [w_in[:, o[i]:o[i + 1]] for i in range(6)]
    blocks = [ka, va]
    blocks += [np.concatenate([ks[:, g * 64:(g + 1) * 64]] * 2, 1) for g in range(HKV)]
    blocks += [np.concatenate([vs[:, g * 64:(g + 1) * 64]] * 2, 1) for g in range(HKV)]
    blocks += [qa, qs]
    sh = {"wA": blockify(np.concatenate(blocks, 1), KC)}
    sh["wo0"] = blockify(f(inp["even_w_out"])[0], KC)
    sh["wo1"] = blockify(f(inp["odd_w_out"])[0], KC)
    for l in range(2):
        sh[f"w1_{l}"] = blockify(f(inp["mlp_w1"])[l], KC)
        sh[f"w2_{l}"] = blockify(f(inp["mlp_w2"])[l], c.KG2)
    sh["wB"] = blockify(f(inp["odd_w_in"])[0], KC)
    gw = f(inp["odd_gate_w"])[0]
    sh["wg"] = np.concatenate([blockify(gw[gi, n], 2) for gi in range(2) for n in range(c.NB)], 0)
    vec = np.zeros((128, c.NV), np.float32)

    def put(name, arr):
        o_, n_ = c.cols[name]
        vec[:, o_:o_ + n_] = arr
    for l in range(2):
        put(f"g{l}_premix", fm(f(inp["pre_mix_g"])[l], KC))
        put(f"g{l}_postmix", fm(f(inp["post_mix_g"])[l], KC))
        put(f"g{l}_premlp", fm(f(inp["pre_mlp_g"])[l], KC))
        put(f"g{l}_postmlp", fm(f(inp["post_mlp_g"])[l], KC))
    for j in range(4):
        put(f"convw{j}", fm(f(inp["odd_conv_w"])[0, j], KC))
    put("convb", fm(f(inp["odd_conv_b"])[0], KC))
    gb = f(inp["odd_gate_b"])[0]
    put("gateb0", fm(gb[0].reshape(-1), KC))
    put("gateb1", fm(gb[1].reshape(-1), KC))
    put("lrul", fm(f(inp["odd_lru_lambda"])[0], KC))
    put("subg", f(inp["even_subln_g"])[0].reshape(128, 1))
    put("sinks", np.broadcast_to(f(inp["even_sinks"])[0][None, :], (128, HQ)))
    for nm, key in (("lq1", "even_lam_q1"), ("lk1", "even_lam_k1"), ("lq2", "even_lam_q2"), ("lk2", "even_lam_k2")):
        put(nm, np.broadcast_to(f(inp[key])[0][None, :], (128, 64)))
    k = np.arange(128)[:, None]
    q = np.arange(512)[None, :]
    m = [(o_ * 128 + k <= q) for o_ in range(4)]
    q1 = np.arange(128)[None, :]
    prev, cur = (k > q1), (k <= q1)
    m.append(np.concatenate([prev, cur, prev, cur], 1))
    sh["masks"] = np.concatenate(m, 1).astype(np.float32)
    return sh, vec


def run(cfg, inp, nc=None):
    c = cfg
    x = np.asarray(inp["x"], np.float32)
    B, S, D = x.shape
    T = c.T
    assert S == 2 * T and B == 4 and D == c.D
    sh, vec = prep_shared(c, inp)
    in_maps = []
    order = ["wA", "wo0", "w1_0", "w2_0", "wB", "wg", "wo1", "w1_1", "w2_1"]
    wall = np.ascontiguousarray(np.concatenate([sh[k_].reshape(-1) for k_ in order]).reshape(-1, 2048))
    for core in range(8):
        b, half = core // 2, core % 2
        xT = np.zeros((D, 2 * T), np.float32)
        if half == 0:
            xT[:, T:] = x[b, :T].T
        else:
            xT[:, :] = x[b].T
        v = vec.copy()
        o_, _ = c.cols["flag"]
        v[:, o_] = float(half)
        m = {"wall": wall, "masks": sh["masks"]}
        m["xT"] = xT
        m["vecs"] = v
        in_maps.append(m)
    if nc is None:
        nc = build(c)
    res = run_bass_kernel_spmd(nc, in_maps, core_ids=list(range(8)))
    out = np.zeros((B, S, D), np.float32)
    for core in range(8):
        b, half = core // 2, core % 2
        out[b, half * T:(half + 1) * T, :] = res.results[core]["yT"].T
    return out


def kernel(**inputs):
    return run(Cfg(), inputs)
```
